# Optimizing a Trainium2 kernel written in Bass

```python
import jax, jax.numpy as jnp
from jax import lax
import numpy as np

D_MODEL = 1024
BATCH = 4
SEQ = 4096
DEPTH = 4

N_META = 16
D_FF = 4 * D_MODEL
D_CONV = D_MODEL // 2
CONV_WIDTH = 31
N_POOL_GROUPS = 4
POOL_WINDOWS = (2, 4, 8, 16)
D_POOL = D_MODEL // 2
POOL_GROUP_DIM = D_POOL // N_POOL_GROUPS
D_EVEN_IN = 2 * D_CONV + D_POOL
HGRN_HEAD_DIM = 128
HGRN_HEADS = D_MODEL // HGRN_HEAD_DIM
D_HGRN = HGRN_HEADS * HGRN_HEAD_DIM
CHUNK = 64
N_EVEN = (DEPTH + 1) // 2
N_ODD = DEPTH // 2
EPS = 1e-6

kernel_name = 'hybrid_conv_pool_hgrn2_trunk'


def _rmsnorm(x, g):
    xf = x.astype(jnp.float32)
    y = xf * lax.rsqrt(jnp.mean(xf * xf, axis=-1, keepdims=True) + EPS)
    return (y * g.astype(jnp.float32)).astype(x.dtype)


def _layernorm(x, g, b):
    xf = x.astype(jnp.float32)
    mu = jnp.mean(xf, axis=-1, keepdims=True)
    xc = xf - mu
    y = xc * lax.rsqrt(jnp.mean(xc * xc, axis=-1, keepdims=True) + EPS)
    return (y * g.astype(jnp.float32) + b.astype(jnp.float32)).astype(x.dtype)


def _conv_mixer(val, gate, conv_w, conv_b, ln_g, ln_b):
    a = val * jax.nn.sigmoid(gate)
    y = lax.conv_general_dilated(
        a, conv_w[:, None, :].astype(a.dtype), window_strides=(1,),
        padding=[(CONV_WIDTH - 1, 0)], dimension_numbers=('NWC', 'WIO', 'NWC'),
        feature_group_count=D_CONV) + conv_b
    return jax.nn.silu(_layernorm(y, ln_g, ln_b))


def _causal_window_mean(x, w):
    L = x.shape[1]
    cs = jnp.cumsum(x.astype(jnp.float32), axis=1)
    cs0 = jnp.pad(cs, ((0, 0), (1, 0), (0, 0)))
    lower = jnp.pad(cs0[:, :L + 1 - w], ((0, 0), (w - 1, 0), (0, 0)))
    count = jnp.minimum(jnp.arange(1, L + 1, dtype=jnp.float32), float(w))
    return ((cs - lower) / count[None, :, None]).astype(x.dtype)


def _pool_mixer(u, pool_w, pool_b, pool_scale):
    Bn, L, _ = u.shape
    ug = u.reshape(Bn, L, N_POOL_GROUPS, POOL_GROUP_DIM)
    pooled = jnp.stack([_causal_window_mean(ug[:, :, gi], w) for gi, w in enumerate(POOL_WINDOWS)], axis=2)
    y = jnp.einsum('blgc,gcd->blgd', pooled - ug, pool_w) + pool_b
    return y.reshape(Bn, L, D_POOL) * pool_scale


def _hgrn2_chunk_scan(q, k, v, logf):
    C = q.shape[3]
    causal = jnp.tril(jnp.ones((C, C), dtype=bool))

    def step(S, inp):
        qc, kc, vc, lfc = inp
        b = jnp.cumsum(lfc, axis=2)
        diff = b[:, :, :, None, :] - b[:, :, None, :, :]
        decay = jnp.exp(jnp.where(causal[:, :, None], diff, -jnp.inf))
        scores = jnp.einsum('bhtk,bhsk,bhtsk->bhts', qc, kc, decay)
        o = (jnp.einsum('bhts,bhsv->bhtv', scores, vc)
             + jnp.einsum('bhtk,bhkv->bhtv', qc * jnp.exp(b), S))
        b_last = b[:, :, -1:, :]
        S = (jnp.exp(b_last[:, :, 0, :])[..., None] * S
             + jnp.einsum('bhsk,bhsv->bhkv', kc * jnp.exp(b_last - b), vc))
        return S, o

    S0 = jnp.zeros((q.shape[1], q.shape[2], q.shape[4], v.shape[4]), jnp.float32)
    _, o = lax.scan(step, S0, (q, k, v, logf))
    return o


def _hgrn2_mixer(u, lb, gnorm_g):
    Bn, L, _ = u.shape
    q, f, i, g = jnp.split(u, 4, axis=-1)
    q = jax.nn.silu(q.astype(jnp.float32))
    forget = lb + (1.0 - lb) * jax.nn.sigmoid(f.astype(jnp.float32))
    k = 1.0 - forget
    logf = jnp.log(forget)
    v = i.astype(jnp.float32)
    pad = CHUNK - N_META
    Lp = L + pad
    n_chunks = Lp // CHUNK

    def to_chunks(t):
        t = jnp.pad(t, ((0, 0), (pad, 0), (0, 0)))
        t = t.reshape(Bn, n_chunks, CHUNK, HGRN_HEADS, HGRN_HEAD_DIM)
        return t.transpose(1, 0, 3, 2, 4)

    o = _hgrn2_chunk_scan(to_chunks(q), to_chunks(k), to_chunks(v), to_chunks(logf))
    o = o.transpose(1, 0, 3, 2, 4).reshape(Bn, Lp, HGRN_HEADS, HGRN_HEAD_DIM)[:, pad:]
    gh = g.reshape(Bn, L, HGRN_HEADS, HGRN_HEAD_DIM).astype(jnp.float32)
    o = _rmsnorm(o, gnorm_g) * jax.nn.silu(gh)
    return o.reshape(Bn, L, D_HGRN).astype(u.dtype)


def setup_inputs(seed: int = 0) -> dict:
    key = jax.random.key(seed)
    ks = jax.random.split(key, 24)
    f32 = jnp.float32
    nrm = lambda k, shape, s: jax.random.normal(k, shape, f32) * s
    return {
        'x': nrm(ks[0], (BATCH, SEQ, D_MODEL), 1.0),
        'meta_tokens': nrm(ks[1], (N_META, D_MODEL), 1.0),
        'mix_norm_g': 1.0 + nrm(ks[2], (DEPTH, D_MODEL), 0.02),
        'mlp_norm_g': 1.0 + nrm(ks[3], (DEPTH, D_MODEL), 0.02),
        'final_norm_g': 1.0 + nrm(ks[4], (D_MODEL,), 0.02),
        'ev_w_in': nrm(ks[5], (N_EVEN, D_MODEL, D_EVEN_IN), D_MODEL ** -0.5),
        'ev_conv_w': nrm(ks[6], (N_EVEN, CONV_WIDTH, D_CONV), CONV_WIDTH ** -0.5),
        'ev_conv_b': nrm(ks[7], (N_EVEN, D_CONV), 0.01),
        'ev_ln_g': 1.0 + nrm(ks[8], (N_EVEN, D_CONV), 0.02),
        'ev_ln_b': nrm(ks[9], (N_EVEN, D_CONV), 0.01),
        'ev_pool_w': nrm(ks[10], (N_EVEN, N_POOL_GROUPS, POOL_GROUP_DIM, POOL_GROUP_DIM), POOL_GROUP_DIM ** -0.5),
        'ev_pool_b': nrm(ks[11], (N_EVEN, N_POOL_GROUPS, POOL_GROUP_DIM), 0.01),
        'ev_pool_scale': 1.0 + nrm(ks[12], (N_EVEN, D_POOL), 0.02),
        'ev_w_out': nrm(ks[13], (N_EVEN, D_CONV + D_POOL, D_MODEL), (D_CONV + D_POOL) ** -0.5),
        'od_w_in': nrm(ks[14], (N_ODD, D_MODEL, 4 * D_HGRN), D_MODEL ** -0.5),
        'od_gnorm_g': 1.0 + nrm(ks[15], (N_ODD, HGRN_HEAD_DIM), 0.02),
        'od_w_out': nrm(ks[16], (N_ODD, D_HGRN, D_MODEL), D_HGRN ** -0.5),
        'lb_param': nrm(ks[17], (DEPTH, D_HGRN), 1.0),
        'mlp_w1': nrm(ks[18], (DEPTH, D_MODEL, D_FF), D_MODEL ** -0.5),
        'mlp_w2': nrm(ks[19], (DEPTH, D_FF, D_MODEL), D_FF ** -0.5),
    }


def reference(x, meta_tokens, mix_norm_g, mlp_norm_g, final_norm_g,
              ev_w_in, ev_conv_w, ev_conv_b, ev_ln_g, ev_ln_b,
              ev_pool_w, ev_pool_b, ev_pool_scale, ev_w_out,
              od_w_in, od_gnorm_g, od_w_out, lb_param, mlp_w1, mlp_w2):
    Bn = x.shape[0]
    meta = jnp.broadcast_to(meta_tokens[None].astype(x.dtype), (Bn, N_META, D_MODEL))
    h = jnp.concatenate([meta, x], axis=1)
    lb_all = jnp.cumsum(jax.nn.softmax(lb_param.astype(jnp.float32), axis=0), axis=0)
    lb_all = lb_all - lb_all[0]
    for layer in range(DEPTH):
        j = layer // 2
        n = _rmsnorm(h, mix_norm_g[layer])
        if layer % 2 == 0:
            u = n @ ev_w_in[j]
            val, gate, pin = jnp.split(u, [D_CONV, 2 * D_CONV], axis=-1)
            ya = _conv_mixer(val, gate, ev_conv_w[j], ev_conv_b[j], ev_ln_g[j], ev_ln_b[j])
            yb = _pool_mixer(pin, ev_pool_w[j], ev_pool_b[j], ev_pool_scale[j])
            h = h + jnp.concatenate([ya, yb], axis=-1) @ ev_w_out[j]
        else:
            u = n @ od_w_in[j]
            y = _hgrn2_mixer(u, lb_all[layer], od_gnorm_g[j])
            h = h + y @ od_w_out[j]
        n = _rmsnorm(h, mlp_norm_g[layer])
        h = h + jnp.square(jax.nn.relu(n @ mlp_w1[layer])) @ mlp_w2[layer]
    return _rmsnorm(h, final_norm_g)[:, N_META:]
```

```python
import numpy as np
from contextlib import ExitStack

import concourse.bass as bass
import concourse.mybir as mybir
from concourse.bass_utils import run_bass_kernel_spmd

F32 = mybir.dt.float32
BF16 = mybir.dt.bfloat16
I32 = mybir.dt.int32
AF = mybir.ActivationFunctionType
ALU = mybir.AluOpType

D = 1024
KC = 8
NMETA = 16
SEQ = 4096
BATCH = 4
EPS = 1e-6
TT = 256
PRE = 64
NCORES = 4
NT_FULL = SEQ // TT
TTOT = PRE + SEQ
WINS = (2, 4, 8, 16)

C_ID = 0
C_CAUS = 128
C_BAND = C_CAUS + 512
CST_W = C_BAND + 16 * 128

PV_EV = 0
PV_EV_W = 144
PV_OD = 2 * PV_EV_W
PV_LB = PV_OD + 2
PV_W = PV_LB + 32


class Res:
    __slots__ = ("name", "last_w", "rd_eng", "rd_dma", "sem", "ndma")
    fence = None

    def __init__(self, name, fenced=False):
        self.name = name
        self.last_w = Res.fence if fenced else None
        self.rd_eng = {}
        self.rd_dma = []
        self.sem = None
        self.ndma = 0


class Op:
    __slots__ = ("eng", "fn", "reads", "writes", "dma", "idx", "need", "tok", "waits", "inc",
                 "attach")


class Prog:
    ENGS = ("pe", "act", "dve", "pool", "sp")

    def __init__(self, nc, es):
        self.nc = nc
        self.es = es
        self.ops = []
        self.nsem = 0

    def add(self, eng, fn, reads=(), writes=(), dma=False, attach=None):
        op = Op()
        op.attach = (eng != "pe") if attach is None else attach
        op.eng = eng
        op.fn = fn
        op.reads = [r for r in reads if r is not None]
        op.writes = [r for r in writes if r is not None]
        op.dma = dma
        op.idx = len(self.ops)
        op.tok = None
        op.waits = []
        op.inc = False
        self.ops.append(op)
        return op

    def fence(self, res_list, fn):
        op = self.add("dve", fn, (), list(res_list))
        Res.fence = op.idx
        return op

    def _newsem(self, name):
        self.nsem += 1
        return self.es.enter_context(self.nc.semaphore(name))

    def finalize(self):
        ops = self.ops
        for op in ops:
            deps = {}

            def dep(i, hard):
                if i is not None and i != op.idx:
                    deps[i] = deps.get(i, False) or hard

            for r in op.reads:
                dep(r.last_w, True)
            for r in op.writes:
                dep(r.last_w, True)
                for i in r.rd_eng.values():
                    dep(i, False)
                for i in r.rd_dma:
                    dep(i, False)
            for r in op.reads:
                if op.dma:
                    r.rd_dma.append(op.idx)
                else:
                    r.rd_eng[op.eng] = op.idx
            for r in op.writes:
                r.last_w = op.idx
                r.rd_eng = {}
                r.rd_dma = []
            need = []
            for i, hard in deps.items():
                p = ops[i]
                if p.eng == op.eng and not p.dma and not op.dma:
                    if op.eng == "pe":
                        continue
                need.append(i)
                p.inc = True
            need.sort()
            op.need = need
        engsem = {e: self._newsem("s_" + e) for e in ("pe", "act", "dve", "pool")}
        cnt = {e: 0 for e in engsem}
        for op in ops:
            if op.dma:
                r = op.writes[0]
                if r.sem is None:
                    r.sem = self._newsem("d_" + r.name)
                r.ndma += 1
                op.tok = (r.sem, 16 * r.ndma)
            elif op.inc:
                cnt[op.eng] += 1
                op.tok = (engsem[op.eng], cnt[op.eng])
        known = {e: {} for e in self.ENGS}
        frozen = {e: None for e in self.ENGS}
        snap = {}
        for op in ops:
            k = known[op.eng]
            for i in op.need:
                p = ops[i]
                s, v = p.tok
                sid = id(s)
                if k.get(sid, (None, 0))[1] >= v:
                    continue
                op.waits.append((s, v))
                base, own = snap[i]
                for sid2, sv in base.items():
                    if k.get(sid2, (None, 0))[1] < sv[1]:
                        k[sid2] = sv
                k[sid] = (s, v)
                frozen[op.eng] = None
            if op.tok is not None:
                if frozen[op.eng] is None:
                    frozen[op.eng] = dict(k)
                snap[op.idx] = (frozen[op.eng], op.tok)
        self.stats = {e: sum(1 for o in ops if o.eng == e) for e in self.ENGS}
        self.nwaits = {e: sum(len(o.waits) for o in ops if o.eng == e) for e in self.ENGS}

    def emit(self, eng, e):
        for op in self.ops:
            if op.eng != eng:
                continue
            waits = op.waits
            if op.fn is None:
                for s, v in waits:
                    e.wait_ge(s, v)
                continue
            if op.attach and waits:
                for s, v in waits[:-1]:
                    e.wait_ge(s, v)
                ins = op.fn(e)
                ins._wait_ge(waits[-1][0], waits[-1][1])
            else:
                for s, v in waits:
                    e.wait_ge(s, v)
                ins = op.fn(e)
            if op.tok is not None:
                ins.then_inc(op.tok[0], 16 if op.dma else 1)


class Buf:
    def __init__(self, t, name):
        self.t = t
        self.r = Res(name)

    def __getitem__(self, key):
        return self.t[key]


class Ring:
    def __init__(self, bufs):
        self.bufs = bufs
        self.i = 0

    def next(self):
        b = self.bufs[self.i % len(self.bufs)]
        self.i += 1
        return b


def build_program(phase_list=None, debug_out=False):
    nc = bass.Bass("TRN2", target_bir_lowering=False)
    es = ExitStack()
    P = Prog(nc, es)
    Res.fence = None

    def din(name, shape):
        return nc.dram_tensor(name, list(shape), F32, kind="ExternalInput").ap()

    hin = din("hin", [TTOT, D])
    gam = din("gam", [9, D])
    cst = din("cst", [128, CST_W])
    pvec = din("pvec", [128, PV_W])
    ev_w_in = din("ev_w_in", [2, D, 1536])
    ev_pool_w = din("ev_pool_w", [2, 4, 128, 128])
    ev_w_out = din("ev_w_out", [2, D, D])
    od_w_in = din("od_w_in", [2, D, 4096])
    od_w_out = din("od_w_out", [2, D, D])
    mlp_w1 = din("mlp_w1", [4, D, 4096])
    mlp_w2 = din("mlp_w2", [4, 4096, D])
    if debug_out:
        out = nc.dram_tensor("out", [TTOT, D], F32, kind="ExternalOutput").ap()
    else:
        out = nc.dram_tensor("out", [SEQ, D], F32, kind="ExternalOutput").ap()
    hbuf = nc.dram_tensor("hbuf", [TTOT, D], F32, kind="Internal").ap()

    ntiles = 1 + NT_FULL
    hbufR = [Res("hb%d" % j) for j in range(ntiles)]
    outR = [Res("ob%d" % j) for j in range(ntiles)]
    out_written = []

    def tinfo(j):
        if j == 0:
            return 0, PRE, 64, 1
        return PRE + (j - 1) * TT, TT, 128, TT // 128

    def sb(name, shape, dt):
        return Buf(es.enter_context(nc.sbuf_tensor(name, list(shape), dt)), name)

    U = [sb("U%d" % i, [128, 4096], BF16) for i in range(16)]
    hT = Ring([sb("hT%d" % i, [128, 2, D], F32) for i in range(2)])
    nT = Ring([sb("nT%d" % i, [128, KC, TT], BF16) for i in range(2)])
    ntok = sb("ntok", [128, 2, D], BF16)
    gbc = sb("gbc", [128, D], F32)
    band = sb("band", [128, 2048], BF16)
    causm = sb("causm", [128, 512], mybir.dt.uint8)
    pv = sb("pv", [128, PV_W], F32)
    ident = sb("ident", [128, 128], BF16)
    identh = sb("identh", [128, 128], BF16)
    ones32 = sb("ones32", [128, 128], F32)
    hmask = sb("hmask", [128, 4], F32)
    st = Ring([sb("st%d" % i, [128, 8], F32) for i in range(2)])
    psum = [Buf(es.enter_context(nc.psum_tensor("ps%d" % i, [128, 512], F32)), "ps%d" % i)
            for i in range(8)]
    G = [sb("G%d" % i, [128, 2048], BF16) for i in range(8)]
    Sst = sb("Sst", [128, 8, 128], F32)
    Spb = sb("Spb", [128, 8, 128], BF16)
    small = sb("small", [128, 128], F32)
    fz = sb("fz", [128, 1], F32)
    rbufs = [sb("rb%d" % i, [128, TT], BF16) for i in range(2)]

    def v3(ap, a):
        return ap.rearrange("p (a b) -> p a b", a=a)

    phase_scratch = []

    def carve(g, off, nbytes, dt, name):
        ap = G[g].t[:, off // 2:(off + nbytes) // 2]
        if dt != BF16:
            ap = ap.bitcast(dt)
        b = Buf.__new__(Buf)
        b.t = ap
        b.r = Res(name, fenced=True)
        phase_scratch.append(b.r)
        return b

    def end_phase():
        rl = list(phase_scratch)
        del phase_scratch[:]
        P.fence(rl + [fz.r], lambda e: e.memset(fz[:, :], 0.0))

    def mm(out_ap, lhsT, rhs, start, stop, R, W, sgc=False):
        if sgc:
            P.add("pe", lambda e: e.matmul(out_ap, lhsT, rhs, start=start, stop=stop,
                                           skip_group_check=True), R, W)
        else:
            P.add("pe", lambda e: e.matmul(out_ap, lhsT, rhs, start=start, stop=stop), R, W)

    def tr(out_ap, in_ap, idn, R, W):
        P.add("pe", lambda e: e.transpose(out_ap, in_ap, idn), R, W)

    def act(out_ap, in_ap, func, R, W, bias=None, scale=None, accum=None):
        kw = {}
        if bias is not None:
            kw["bias"] = bias
        if scale is not None:
            kw["scale"] = scale
        if accum is not None:
            kw["accum_out"] = accum
        P.add("act", lambda e: e.activation(out_ap, in_ap, func, **kw), R, W,
              attach=(accum is None))

    def tt(eng, out_ap, a, b, op, R, W):
        P.add(eng, lambda e: e.tensor_tensor(out_ap, a, b, op), R, W)

    def ts(eng, out_ap, a, s1, s2, op0, op1, R, W):
        if op1 is None:
            P.add(eng, lambda e: e.tensor_scalar(out_ap, a, s1, None, op0), R, W)
        else:
            P.add(eng, lambda e: e.tensor_scalar(out_ap, a, s1, s2, op0, op1), R, W)

    def stt(out_ap, a, sc, b, op0, op1, R, W):
        P.add("dve", lambda e: e.scalar_tensor_tensor(out_ap, a, sc, b, op0, op1), R, W)

    def cp(eng, out_ap, in_ap, R, W):
        if eng == "act":
            P.add("act", lambda e: e.copy(out_ap, in_ap), R, W)
        else:
            P.add(eng, lambda e: e.tensor_copy(out_ap, in_ap), R, W)

    def memset(eng, ap, val, W):
        P.add(eng, lambda e: e.memset(ap, val), (), W)

    def recip(out_ap, in_ap, R, W):
        P.add("dve", lambda e: e.reciprocal(out_ap, in_ap), R, W)

    def dma(q, out_ap, in_ap, R, Wres):
        P.add(q, lambda e: e.dma_start(out=out_ap, in_=in_ap), R, [Wres], dma=True)

    pA = Ring(psum[0:3])
    pB = Ring(psum[3:6])
    pT = Ring(psum[6:8])

    c0 = G[0][:, 0:2 * C_BAND].bitcast(F32)
    dma("sp", c0[:, :], cst[:, 0:C_BAND], [], G[0].r)
    dma("sp", pv[:, :], pvec[:, :], [], pv.r)
    cp("dve", ident[:, :], c0[:, C_ID:C_ID + 128], [G[0].r], [ident.r])
    ts("dve", identh[:, :], c0[:, C_ID:C_ID + 128], 0.5, None, ALU.mult, None, [G[0].r],
       [identh.r])
    cp("dve", causm[:, :], c0[:, C_CAUS:C_CAUS + 512], [G[0].r], [causm.r])
    for i in range(2):
        ci = G[1 + i][:, :].bitcast(F32)
        dma("sp", ci[:, :], cst[:, C_BAND + 1024 * i:C_BAND + 1024 * (i + 1)], [], G[1 + i].r)
        cp("dve", band[:, 1024 * i:1024 * (i + 1)], ci[:, :], [G[1 + i].r], [band.r])
    memset("dve", ones32[:, :], 1.0, [ones32.r])
    memset("dve", hmask[:, :], 1.0, [hmask.r])
    memset("dve", hmask[0:48, 1:2], 0.0, [hmask.r])
    memset("dve", hmask[:, 2:3], EPS, [hmask.r])
    epsc = hmask[:, 2:3]
    P.fence([G[0].r, G[1].r, G[2].r, fz.r], lambda e: e.memset(fz[:, :], 0.0))

    def load_tile(j, src, srcR):
        t0, tw, rows, nsub = tinfo(j)
        h = hT.next()
        dma("sp", h[:rows, :nsub, :], src[t0:t0 + tw, :].rearrange("(s p) d -> p s d", p=rows),
            [srcR[j]] if srcR is not None else [], h.r)
        return h

    def tile_rstd(j, h):
        t0, tw, rows, nsub = tinfo(j)
        s = st.next()
        for q in range(nsub):
            act(ntok[:rows, q, :], h[:rows, q, :], AF.Square, [h.r], [ntok.r, s.r],
                accum=s[:rows, q:q + 1])
        act(s[:rows, 4:4 + nsub], s[:rows, 0:nsub], AF.Ln, [s.r, hmask.r], [s.r],
            bias=epsc[:rows, :], scale=1.0 / D)
        act(s[:rows, 6:6 + nsub], s[:rows, 4:4 + nsub], AF.Exp, [s.r], [s.r], scale=-0.5)
        return s

    def head_a(j, src, srcR):
        t0, tw, rows, nsub = tinfo(j)
        h = load_tile(j, src, srcR)
        s = tile_rstd(j, h)
        for q in range(nsub):
            stt(ntok[:rows, q, :], h[:rows, q, :], s[:rows, 6 + q:7 + q], gbc[:rows, :],
                ALU.mult, ALU.mult, [h.r, s.r, gbc.r], [ntok.r])
        return h

    def head_b(j, h):
        t0, tw, rows, nsub = tinfo(j)
        n = nT.next()
        for q in range(nsub):
            ps = pT.next()
            psb = v3(ps[:, :].bitcast(BF16), 8)
            for c in range(KC):
                tr(psb[:, c, :rows], ntok[:rows, q, c * 128:(c + 1) * 128], ident[:rows, :rows],
                   [ntok.r, ident.r], [ps.r])
            cp("act", n[:, :, q * 128:q * 128 + rows], psb[:, :, :rows], [ps.r], [n.r])
        return h, n

    def head(j, src, srcR):
        return head_b(j, head_a(j, src, srcR))

    def finish_tile(j, h, dst, dstR):
        t0, tw, rows, nsub = tinfo(j)
        dma("sp", dst[t0:t0 + tw, :].rearrange("(s p) d -> p s d", p=rows), h[:rows, :nsub, :],
            [h.r], dstR[j])

    def resid_add(j, h, q, half, ps):
        t0, tw, rows, nsub = tinfo(j)
        mcol = 1 if j == 0 else 0
        stt(h[:rows, q, half * 512:(half + 1) * 512], ps[:rows, :], hmask[:rows, mcol:mcol + 1],
            h[:rows, q, half * 512:(half + 1) * 512], ALU.mult, ALU.add,
            [ps.r, hmask.r, h.r], [h.r])

    def load_gamma(row):
        dma("sp", gbc[:, :], gam[row:row + 1, :].broadcast_to([128, D]), [], gbc.r)

    def pipeline(src, srcR, stageB):
        heads = {0: head(0, src, srcR)}

        for j in range(ntiles):
            part = {}

            def mid(stage=1, j=j, part=part):
                if j + 1 >= ntiles:
                    return None
                if (j + 1) in heads:
                    return heads[j + 1]
                if "a" not in part:
                    part["a"] = head_a(j + 1, src, srcR)
                if stage == 0:
                    return None
                heads[j + 1] = head_b(j + 1, part["a"])
                return heads[j + 1]
            stageB(j, heads[j][0], heads[j][1], mid)
            mid(1)
            del heads[j]

    def mlp_phase(L, src, srcR, dst, dstR):
        W1v = [v3(U[u][:, :], 8) for u in range(8)]
        W2v = [v3(U[8 + u][:, :], 4) for u in range(8)]
        load_gamma(4 + L)
        for u in range(8):
            dma("pool", W1v[u], mlp_w1[L, :, u * 512:(u + 1) * 512].rearrange(
                "(kc p) n -> p kc n", p=128), [], U[u].r)
        for u in range(8):
            dma("pool", W2v[u], mlp_w2[L, u * 512:(u + 1) * 512, :].rearrange(
                "(hc p) n -> p hc n", p=128), [], U[8 + u].r)
        hidb = [carve(i, 0, 4096, BF16, "hid%d" % i) for i in range(4)]
        hidv = [v3(b.t, 8) for b in hidb]
        hidR = [b.r for b in hidb]
        rr = Ring([(b[:, :], b.r) for b in rbufs])

        def stageB(j, h, n, mid):
            t0, tw, rows, nsub = tinfo(j)
            for hc in range(32):
                if hc == 2:
                    mid(0)
                if hc == 14:
                    mid(1)
                ps = pA.next()
                for kc in range(KC):
                    mm(ps[:, :tw], W1v[hc // 4][:, kc, (hc % 4) * 128:(hc % 4 + 1) * 128],
                       n[:, kc, :tw], kc == 0, kc == KC - 1, [U[hc // 4].r, n.r], [ps.r])
                r_ap, r_res = rr.next()
                act(r_ap[:, :tw], ps[:, :tw], AF.Relu, [ps.r], [r_res])
                tt("pool", hidv[hc // 8][:, hc % 8, :tw], r_ap[:, :tw], r_ap[:, :tw], ALU.mult,
                   [r_res], [hidR[hc // 8]])
            for q in range(nsub):
                for half in range(2):
                    ps = pB.next()
                    for hc in range(32):
                        mm(ps[:rows, :], hidv[hc // 8][:, hc % 8, q * 128:q * 128 + rows],
                           W2v[hc // 4][:, hc % 4, half * 512:(half + 1) * 512],
                           hc == 0, hc == 31, [hidR[hc // 8], U[8 + hc // 4].r], [ps.r])
                    resid_add(j, h, q, half, ps)
            finish_tile(j, h, dst, dstR)

        pipeline(src, srcR, stageB)
        end_phase()

    def lb_setup():
        lbp = pv[:, PV_LB:PV_LB + 32].rearrange("p (h l) -> p h l", l=4)
        tmp = small[:, 56:64]
        e = U[13][:, 0:64].bitcast(F32).rearrange("p (h l) -> p h l", l=4)
        P.add("dve", lambda en: en.tensor_reduce(tmp, lbp, mybir.AxisListType.X, ALU.max),
              [pv.r], [small.r])
        tt("dve", e, lbp, small[:, 56:64].unsqueeze(2).broadcast_to([128, 8, 4]), ALU.subtract,
           [pv.r, small.r], [U[13].r])
        act(e, e, AF.Exp, [U[13].r], [U[13].r])
        P.add("dve", lambda en: en.tensor_reduce(tmp, e, mybir.AxisListType.X, ALU.add),
              [U[13].r], [small.r])
        recip(tmp, tmp, [small.r], [small.r])
        tt("dve", e, e, small[:, 56:64].unsqueeze(2).broadcast_to([128, 8, 4]), ALU.mult,
           [U[13].r, small.r], [U[13].r])
        cp("dve", small[:, 0:8], e[:, :, 1], [U[13].r], [small.r])
        tt("dve", small[:, 24:32], e[:, :, 1], e[:, :, 2], ALU.add, [U[13].r], [small.r])
        tt("dve", small[:, 24:32], small[:, 24:32], e[:, :, 3], ALU.add, [U[13].r, small.r],
           [small.r])
        for base in (0, 24):
            ts("dve", small[:, base + 8:base + 16], small[:, base:base + 8], -0.5, 0.5,
               ALU.mult, ALU.add, [small.r], [small.r])
            ts("dve", small[:, base + 16:base + 24], small[:, base:base + 8], 0.5, -0.5,
               ALU.mult, ALU.add, [small.r], [small.r])
            ts("dve", small[:, base:base + 8], small[:, base:base + 8], 0.5, 0.5,
               ALU.mult, ALU.add, [small.r], [small.r])

    def even_phase(L, src, srcR, dst, dstR):
        jl = L // 2
        pb = PV_EV + jl * PV_EV_W
        Win = [v3(U[u][:, :], 8) for u in range(3)]
        Wout = [v3(U[8 + u][:, :], 4) for u in range(2)]
        diag = [v3(U[10 + g][:, :], 32) for g in range(4)]
        poolw = v3(U[3][:, 0:512], 4)
        load_gamma(L)
        for u in range(3):
            dma("pool", Win[u], ev_w_in[jl, :, u * 512:(u + 1) * 512].rearrange(
                "(kc p) n -> p kc n", p=128), [], U[u].r)
        dma("pool", poolw, ev_pool_w[jl].rearrange("g c d -> c g d"), [], U[3].r)
        for u in range(2):
            dma("pool", Wout[u], ev_w_out[jl, u * 512:(u + 1) * 512, :].rearrange(
                "(kc p) n -> p kc n", p=128), [], U[8 + u].r)
        for g in range(4):
            c = pb + g * 31
            tt("dve", diag[g][:, 0:31, :], identh[:, :].unsqueeze(1).broadcast_to([128, 31, 128]),
               pv[:, c:c + 31].unsqueeze(2).broadcast_to([128, 31, 128]), ALU.mult,
               [identh.r, pv.r], [U[10 + g].r])
        pbs = small[:, 48 + 4 * jl:52 + 4 * jl]
        tt("dve", pbs, pv[:, pb + 136:pb + 140], pv[:, pb + 140:pb + 144], ALU.mult,
           [pv.r], [small.r])
        cb = lambda g: pv[:, pb + 124 + g:pb + 125 + g]
        lng = lambda g: pv[:, pb + 128 + g:pb + 129 + g]
        lnb = lambda g: pv[:, pb + 132 + g:pb + 133 + g]
        psc = lambda g: pv[:, pb + 140 + g:pb + 141 + g]

        AW = 32 + TT
        aT = [carve(i, 0, 4 * AW * 2, BF16, "aT%d" % i) for i in range(2)]
        aTv = [v3(b.t, 4) for b in aT]
        sig = Ring([carve(i, 2560, 1024, F32, "sig%d" % i) for i in range(2)])
        yb_ = carve(2, 0, 4096, F32, "y")
        ysq = carve(3, 0, 4096, F32, "ysq")
        yv, ysqv = v3(yb_.t, 4), v3(ysq.t, 4)
        mean = carve(4, 0, 1024, F32, "mean")
        msq = carve(4, 1024, 1024, F32, "msq")
        var = carve(4, 2048, 1024, F32, "var")
        rstd = carve(4, 3072, 1024, F32, "rstd")
        zz = Ring([carve(5, i * 1024, 1024, F32, "z%d" % i) for i in range(2)])
        pdT = carve(5, 2048, 2048, BF16, "pdT")
        pdTv = v3(pdT.t, 4)
        yT = carve(6, 0, 4096, BF16, "yT")
        yTv = v3(yT.t, 8)
        pinc = carve(7, 0, 2048, BF16, "pinc")
        pincv = v3(pinc.t, 2)
        pinp = carve(7, 2048, 1024, BF16, "pinp")
        memset("dve", aT[1].t[:, :], 0.0, [aT[1].r])

        def bandm(kind, gi):
            c = (kind * 4 + gi) * 128
            return band[:, c:c + 128]

        def stageB(j, h, n, mid):
            t0, tw, rows, nsub = tinfo(j)
            cur, prv = aTv[j % 2], aTv[(j + 1) % 2]
            curR, prvR = aT[j % 2].r, aT[(j + 1) % 2].r
            twp = tinfo(j - 1)[1] if j > 0 else TT
            mid(0)
            cp("pool", cur[:, :, 0:32], prv[:, :, twp:twp + 32], [prvR], [curR])
            for g in range(4):
                psv = pA.next()
                for kc in range(KC):
                    mm(psv[:, :tw], Win[0][:, kc, g * 128:(g + 1) * 128], n[:, kc, :tw],
                       kc == 0, kc == KC - 1, [U[0].r, n.r], [psv.r])
                psg = pA.next()
                for kc in range(KC):
                    mm(psg[:, :tw], Win[1][:, kc, g * 128:(g + 1) * 128], n[:, kc, :tw],
                       kc == 0, kc == KC - 1, [U[1].r, n.r], [psg.r])
                sg_ = sig.next()
                act(sg_.t[:, :tw], psg[:, :tw], AF.Tanh, [psg.r], [sg_.r], scale=0.5)
                stt(cur[:, g, 32:32 + tw], sg_.t[:, :tw], 1.0, psv[:, :tw], ALU.add, ALU.mult,
                    [psv.r, sg_.r], [curR])
            mid()
            for q in range(nsub):
                ps = pB.next()
                for kc in range(KC):
                    mm(ps[:rows, :], n[:, kc, q * 128:q * 128 + rows], Win[2][:, kc, :],
                       kc == 0, kc == KC - 1, [n.r, U[2].r], [ps.r])
                cp("act", pincv[:rows, q, :], ps[:rows, :], [ps.r], [pinc.r])
            for g in range(4):
                psc_ = pA.next()
                for tap in range(31):
                    mm(psc_[:, :tw], diag[g][:, tap, :], cur[:, g, 2 + tap:2 + tap + tw],
                       tap == 0, tap == 30, [U[10 + g].r, curR], [psc_.r])
                act(yv[:, g, :tw], psc_[:, :tw], AF.Identity, [psc_.r, pv.r], [yb_.r], bias=cb(g))
                act(ysqv[:, g, :tw], psc_[:, :tw], AF.Square, [psc_.r, pv.r], [ysq.r], bias=cb(g))
            ps1 = pB.next()
            for g in range(4):
                mm(ps1[:, :tw], ones32[:, :], yv[:, g, :tw], g == 0, g == 3, [ones32.r, yb_.r],
                   [ps1.r])
            ps2 = pB.next()
            for g in range(4):
                mm(ps2[:, :tw], ones32[:, :], ysqv[:, g, :tw], g == 0, g == 3, [ones32.r, ysq.r],
                   [ps2.r])
            ts("dve", mean.t[:, :tw], ps1[:, :tw], 1.0 / 512, None, ALU.mult, None, [ps1.r],
               [mean.r])
            tt("dve", msq.t[:, :tw], mean.t[:, :tw], mean.t[:, :tw], ALU.mult, [mean.r], [msq.r])
            ts("dve", msq.t[:, :tw], msq.t[:, :tw], -1.0, EPS, ALU.mult, ALU.add, [msq.r], [msq.r])
            stt(var.t[:, :tw], ps2[:, :tw], 1.0 / 512, msq.t[:, :tw], ALU.mult, ALU.add,
                [ps2.r, msq.r], [var.r])
            act(var.t[:, :tw], var.t[:, :tw], AF.Ln, [var.r], [var.r])
            act(rstd.t[:, :tw], var.t[:, :tw], AF.Exp, [var.r], [rstd.r], scale=-0.5)
            for g in range(4):
                z = zz.next()
                tt("dve", z.t[:, :tw], yv[:, g, :tw], mean.t[:, :tw], ALU.subtract,
                   [yb_.r, mean.r], [z.r])
                tt("dve", z.t[:, :tw], z.t[:, :tw], rstd.t[:, :tw], ALU.mult, [z.r, rstd.r], [z.r])
                act(yTv[:, g, :tw], z.t[:, :tw], AF.Silu, [z.r, pv.r], [yT.r], bias=lnb(g),
                    scale=lng(g))
            for gi in range(4):
                ps = pB.next()
                for q in range(nsub):
                    oc = ps[:, q * 128:q * 128 + rows]
                    have_prev = not (j == 0 and q == 0)
                    if have_prev:
                        if q > 0:
                            mm(oc, pincv[:, q - 1, gi * 128:(gi + 1) * 128], bandm(1, gi)[:, :rows],
                               True, False, [pinc.r, band.r], [ps.r])
                        elif j == 1:
                            mm(oc, pinp.t[:64, gi * 128:(gi + 1) * 128], bandm(2, gi)[:64, :rows],
                               True, False, [pinp.r, band.r], [ps.r])
                        else:
                            mm(oc, pinp.t[:, gi * 128:(gi + 1) * 128], bandm(1, gi)[:, :rows],
                               True, False, [pinp.r, band.r], [ps.r])
                    dk = 3 if j == 0 else 0
                    mm(oc, pincv[:rows, q, gi * 128:(gi + 1) * 128], bandm(dk, gi)[:rows, :rows],
                       not have_prev, True, [pinc.r, band.r], [ps.r])
                cp("act", pdTv[:, gi, :tw], ps[:, :tw], [ps.r], [pdT.r])
                psl = pB.next()
                mm(psl[:, :tw], poolw[:, gi, :], pdTv[:, gi, :tw], True, True, [U[3].r, pdT.r],
                   [psl.r])
                act(yTv[:, 4 + gi, :tw], psl[:, :tw], AF.Identity, [psl.r, pv.r, small.r], [yT.r],
                    bias=pbs[:, gi:gi + 1], scale=psc(gi))
            cp("pool", pinp.t[:rows, :], pincv[:rows, nsub - 1, :], [pinc.r], [pinp.r])
            for q in range(nsub):
                for half in range(2):
                    ps = pB.next()
                    for kc in range(KC):
                        mm(ps[:rows, :], yTv[:, kc, q * 128:q * 128 + rows],
                           Wout[kc // 4][:, kc % 4, half * 512:(half + 1) * 512],
                           kc == 0, kc == KC - 1, [yT.r, U[8 + kc // 4].r], [ps.r])
                    resid_add(j, h, q, half, ps)
            finish_tile(j, h, dst, dstR)

        pipeline(src, srcR, stageB)
        end_phase()

    def hgrn_phase(L, src, srcR, dst, dstR):
        jl = L // 2
        lbb = 0 if L == 1 else 24
        lbc = lambda h: small[:, lbb + h:lbb + h + 1]
        omlc = lambda h: small[:, lbb + 8 + h:lbb + 9 + h]
        nomlc = lambda h: small[:, lbb + 16 + h:lbb + 17 + h]
        gng = pv[:, PV_OD + jl:PV_OD + jl + 1]
        Win = [v3(U[u][:, :], 8) for u in range(8)]
        Wout = [v3(U[8 + u][:, :], 4) for u in range(2)]
        load_gamma(L)
        for u in (0, 1, 6, 7, 2, 3, 4, 5):
            dma("pool", Win[u], od_w_in[jl, :, u * 512:(u + 1) * 512].rearrange(
                "(kc p) n -> p kc n", p=128), [], U[u].r)
        for u in range(2):
            dma("pool", Wout[u], od_w_out[jl, u * 512:(u + 1) * 512, :].rearrange(
                "(kc p) n -> p kc n", p=128), [], U[8 + u].r)
        X = [v3(U[10 + i][:, :].bitcast(F32), 8) for i in range(5)]
        XR = [U[10 + i].r for i in range(5)]
        qs, sgm, lgf, bb, kk = X
        qsR, sgmR, lgfR, bbR, kkR = XR
        kdT = v3(U[15][:, 0:2048], 8)
        cmask = U[15][:, 2048:2560].bitcast(F32)
        kdTR = U[15].r
        sg = carve(0, 0, 4096, BF16, "sg")
        qt = carve(1, 0, 4096, BF16, "qt")
        kt = carve(2, 0, 4096, BF16, "kt")
        kdtm = carve(3, 0, 4096, BF16, "kdtm")
        vtm = carve(4, 0, 4096, BF16, "vtm")
        yT = carve(5, 0, 4096, BF16, "yT")
        scT = carve(6, 0, 2048, BF16, "scT")
        osq = carve(6, 2048, 2048, F32, "osq")
        rstd = carve(7, 0, 2048, F32, "rstd")
        t1 = carve(7, 2048, 2048, F32, "t1")
        sgv, qtv, ktv, yTv = v3(sg.t, 8), v3(qt.t, 8), v3(kt.t, 8), v3(yT.t, 8)
        kdtmv, vtmv, scTv = v3(kdtm.t, 2), v3(vtm.t, 2), v3(scT.t, 2)
        memset("dve", cmask[:, :], 1.0, [kdTR])
        for c in range(TT // 64):
            memset("dve", cmask[:, c * 64:c * 64 + 1], 0.0, [kdTR])
        memset("dve", scT.t[:, :], 0.0, [scT.r])
        memset("dve", Sst[:, :, :], 0.0, [Sst.r])
        emid_all = small[:, 64:96]
        elast_all = small[:, 96:128]
        early = set()
        pO = Ring([psum[0], psum[1]])
        pP = Ring([psum[2], psum[6]])

        def stageB(j, h, n, mid):
            t0, tw, rows, nsub = tinfo(j)
            nch = tw // 64
            nb = 8 * nch

            def proj(h_, ubase, func, dst_ap, dstR):
                ps = pP.next()
                for kc in range(KC):
                    mm(ps[:, :tw], Win[ubase + h_ // 4][:, kc, (h_ % 4) * 128:(h_ % 4 + 1) * 128],
                       n[:, kc, :tw], kc == 0, kc == KC - 1, [U[ubase + h_ // 4].r, n.r], [ps.r])
                act(dst_ap, ps[:, :tw], func, [ps.r], [dstR])

            def cv(x):
                if nch == 1:
                    return x[:, :, 0:64]
                return x[:, :, :].rearrange("p h (c t) -> p (h c) t", t=64)

            if j in early:
                act(sgm[:, :, :tw], sgm[:, :, :tw], AF.Tanh, [sgmR], [sgmR], scale=0.5)
                act(qs[:, :, :tw], qs[:, :, :tw], AF.Silu, [qsR], [qsR])
            else:
                for h_ in range(8):
                    ps = pP.next()
                    for kc in range(KC):
                        mm(ps[:, :tw], Win[2 + h_ // 4][:, kc, (h_ % 4) * 128:(h_ % 4 + 1) * 128],
                           n[:, kc, :tw], kc == 0, kc == KC - 1, [U[2 + h_ // 4].r, n.r], [ps.r])
                    act(sgm[:, h_, :tw], ps[:, :tw], AF.Tanh, [ps.r], [sgmR], scale=0.5)
                for h_ in range(8):
                    proj(h_, 0, AF.Silu, qs[:, h_, :tw], qsR)
            for h_ in range(8):
                proj(h_, 6, AF.Silu, sgv[:, h_, :tw], sg.r)
            mid(0)
            for h_ in range(8):
                act(lgf[:, h_, :tw], sgm[:, h_, :tw], AF.Ln, [sgmR, small.r], [lgfR],
                    bias=lbc(h_), scale=omlc(h_))
                ts("pool", kk[:, h_, :tw], sgm[:, h_, :tw], nomlc(h_), omlc(h_), ALU.mult, ALU.add,
                   [sgmR, small.r], [kkR])
                P.add("dve", (lambda en, h_=h_: en.tensor_tensor_scan(
                    bb[:, h_, :tw], cmask[:, :tw], lgf[:, h_, :tw], 0.0, ALU.mult, ALU.add)),
                    [kdTR, lgfR], [bbR])
            nxt = mid()
            early_jobs = []
            if nxt is not None:
                tw1 = tinfo(j + 1)[1]
                n1 = nxt[1]
                for h_ in range(8):
                    early_jobs.append((2, h_, sgm, sgmR))
                    early_jobs.append((0, h_, qs, qsR))
                early.add(j + 1)

            def run_early(k):
                for _ in range(k):
                    if not early_jobs:
                        return
                    ub, h_, dstb, dstR_ = early_jobs.pop(0)
                    ps = pP.next()
                    for kc in range(KC):
                        mm(ps[:, :tw1], Win[ub + h_ // 4][:, kc, (h_ % 4) * 128:(h_ % 4 + 1) * 128],
                           n1[:, kc, :tw1], kc == 0, kc == KC - 1, [U[ub + h_ // 4].r, n1.r], [ps.r])
                    act(dstb[:, h_, :tw1], ps[:, :tw1], AF.Identity, [ps.r], [dstR_])
            bv = cv(bb)
            bmid = bv[:, :, 31:32].broadcast_to([128, nb, 64])
            blast = bv[:, :, 63:64].broadcast_to([128, nb, 64])
            emid = emid_all[:, 0:nb]
            elast = elast_all[:, 0:nb]
            for q in range(nsub):
                for half in range(2):
                    ps = pB.next()
                    for kc in range(KC):
                        mm(ps[:rows, :], n[:, kc, q * 128:q * 128 + rows], Win[4 + half][:, kc, :],
                           kc == 0, kc == KC - 1, [n.r, U[4 + half].r], [ps.r])
                    cp("act", vtmv[:rows, q, half * 512:(half + 1) * 512], ps[:rows, :], [ps.r],
                       [vtm.r])
            tt("dve", cv(sgm), bv, bmid, ALU.subtract, [bbR, sgmR], [sgmR])
            tt("dve", cv(lgf), blast, bv, ALU.subtract, [bbR, lgfR], [lgfR])
            act(emid.unsqueeze(2), bv[:, :, 31:32], AF.Exp, [bbR], [small.r])
            act(elast.unsqueeze(2), bv[:, :, 63:64], AF.Exp, [bbR], [small.r])
            act(bv, cv(sgm), AF.Exp, [sgmR, bbR], [bbR])
            act(cv(sgm), cv(sgm), AF.Exp, [sgmR], [sgmR], scale=-1.0)
            act(cv(lgf), cv(lgf), AF.Exp, [lgfR], [lgfR])
            tt("dve", cv(ktv), cv(kk), cv(sgm), ALU.mult, [kkR, sgmR], [kt.r])
            tt("dve", cv(kdT), cv(kk), cv(lgf), ALU.mult, [kkR, lgfR], [kdTR])
            tt("dve", cv(qtv), cv(qs), bv, ALU.mult, [qsR, bbR], [qt.r])
            for q in range(nsub):
                ps = pT.next()
                psb = v3(ps[:, :].bitcast(BF16), 8)
                for h_ in range(8):
                    tr(psb[:rows, h_, :], kdT[:, h_, q * 128:q * 128 + rows], ident[:, :],
                       [kdTR, ident.r], [ps.r])
                cp("act", kdtmv[:rows, q, :], psb[:rows, :, :].rearrange("p h k -> p (h k)"),
                   [ps.r], [kdtm.r])
            for q in range(nsub):
                pa_ = pB.next()
                for h_ in range(8):
                    mm(pa_[0:64, h_ * 64:(h_ + 1) * 64], ktv[:, h_, q * 128:q * 128 + 64],
                       qtv[:, h_, q * 128:q * 128 + 64], True, True, [kt.r, qt.r], [pa_.r])
                P.add("dve", (lambda en, pa_=pa_, q=q: en.copy_predicated(
                    scTv[0:64, q, :], causm[0:64, :], pa_[0:64, :])), [pa_.r, causm.r], [scT.r])
                if rows == 128:
                    pb_ = pB.next()
                    for h_ in range(8):
                        mm(pb_[:, h_ * 64:(h_ + 1) * 64], ktv[:, h_, q * 128:q * 128 + 128],
                           qtv[:, h_, q * 128 + 64:q * 128 + 128], True, True, [kt.r, qt.r],
                           [pb_.r])
                    P.add("dve", (lambda en, pb_=pb_, q=q: en.copy_predicated(
                        scTv[64:128, q, :], causm[64:128, :], pb_[64:128, :])),
                        [pb_.r, causm.r], [scT.r])

            def gnorm(c, pso):
                def run():
                    act(osq.t[:, :], pso[:, :], AF.Square, [pso.r], [osq.r])
                    psm = pB.next()
                    mm(psm[:, :], ones32[:, :], osq.t[:, :], True, True, [ones32.r, osq.r],
                       [psm.r])
                    act(rstd.t[:, :], psm[:, :], AF.Ln, [psm.r, hmask.r], [rstd.r], bias=epsc,
                        scale=1.0 / 128)
                    act(rstd.t[:, :], rstd.t[:, :], AF.Exp, [rstd.r], [rstd.r], scale=-0.5)
                    act(t1.t[:, :], pso[:, :], AF.Identity, [pso.r, pv.r], [t1.r], scale=gng)
                    tt("pool", t1.t[:, :], t1.t[:, :], rstd.t[:, :], ALU.mult, [t1.r, rstd.r],
                       [t1.r])
                    tt("pool", yTv[:, :, c * 64:(c + 1) * 64], v3(t1.t, 8),
                       sgv[:, :, c * 64:(c + 1) * 64], ALU.mult, [t1.r, sg.r], [yT.r])
                return run

            pending = None
            for c in range(nch):
                q, r0 = c // 2, (c % 2) * 64
                pss = [pB.next(), pB.next()]
                for h_ in range(8):
                    mm(pss[h_ // 4][:, (h_ % 4) * 128:(h_ % 4 + 1) * 128],
                       kdtmv[r0:r0 + 64, q, h_ * 128:(h_ + 1) * 128],
                       vtmv[r0:r0 + 64, q, h_ * 128:(h_ + 1) * 128], True, True,
                       [kdtm.r, vtm.r], [pss[h_ // 4].r])
                em = emid.rearrange("p (h c) -> p h c", c=nch)[:, :, c:c + 1]
                tt("dve", Spb[:, :, :], Sst[:, :, :], em.broadcast_to([128, 8, 128]), ALU.mult,
                   [Sst.r, small.r], [Spb.r])
                pso = pO.next()
                for h_ in range(8):
                    mm(pso[:, h_ * 64:(h_ + 1) * 64], vtmv[r0:r0 + 64, q, h_ * 128:(h_ + 1) * 128],
                       scTv[r0:r0 + 64, q, h_ * 64:(h_ + 1) * 64], h_ == 0, False, [vtm.r, scT.r],
                       [pso.r], sgc=True)
                for h_ in range(8):
                    mm(pso[:, h_ * 64:(h_ + 1) * 64], Spb[:, h_, :], qtv[:, h_, c * 64:(c + 1) * 64],
                       False, h_ == 7, [Spb.r, qt.r], [pso.r], sgc=True)
                el = elast.rearrange("p (h c) -> p h c", c=nch)[:, :, c:c + 1]
                tt("dve", Sst[:, :, :], Sst[:, :, :], el.broadcast_to([128, 8, 128]), ALU.mult,
                   [Sst.r, small.r], [Sst.r])
                for k2 in range(2):
                    sv = Sst[:, 4 * k2:4 * k2 + 4, :].rearrange("p h v -> p (h v)")
                    tt("dve", sv, sv, pss[k2][:, :], ALU.add, [Sst.r, pss[k2].r], [Sst.r])
                run_early((16 + nch - 1) // nch)
                if pending is not None:
                    pending()
                pending = gnorm(c, pso)
            run_early(16)
            pending()
            for q in range(nsub):
                for half in range(2):
                    ps = pB.next()
                    for h_ in range(8):
                        mm(ps[:rows, :], yTv[:, h_, q * 128:q * 128 + rows],
                           Wout[h_ // 4][:, h_ % 4, half * 512:(half + 1) * 512],
                           h_ == 0, h_ == 7, [yT.r, U[8 + h_ // 4].r], [ps.r])
                    resid_add(j, h, q, half, ps)
            finish_tile(j, h, dst, dstR)

        pipeline(src, srcR, stageB)
        end_phase()

    def final_phase(src, srcR):
        load_gamma(8)
        for j in range(1, ntiles):
            t0, tw, rows, nsub = tinfo(j)
            h = load_tile(j, src, srcR)
            s = tile_rstd(j, h)
            for q in range(nsub):
                stt(h[:rows, q, :], h[:rows, q, :], s[:rows, 6 + q:7 + q], gbc[:rows, :],
                    ALU.mult, ALU.mult, [h.r, s.r, gbc.r], [h.r])
            o0 = t0 - PRE
            dma("sp", out[o0:o0 + tw, :].rearrange("(s p) d -> p s d", p=rows),
                h[:rows, :nsub, :], [h.r], outR[j])
            out_written.append(outR[j])

    def debug_copy_phase(src, srcR):
        for j in range(ntiles):
            t0, tw, rows, nsub = tinfo(j)
            h = load_tile(j, src, srcR)
            dma("sp", out[t0:t0 + tw, :].rearrange("(s p) d -> p s d", p=rows),
                h[:rows, :nsub, :], [h.r], outR[j])
            out_written.append(outR[j])

    phases = []
    for L in range(4):
        phases.append(("mix", L))
        phases.append(("mlp", L))
    if phase_list is not None:
        phases = list(phase_list)
    src, srcR = hin, None
    if any(k == "mix" and l % 2 == 1 for k, l in phases):
        lb_setup()
    for kind, L in phases:
        if kind == "mlp":
            mlp_phase(L, src, srcR, hbuf, hbufR)
        elif L % 2 == 0:
            even_phase(L, src, srcR, hbuf, hbufR)
        else:
            hgrn_phase(L, src, srcR, hbuf, hbufR)
        src, srcR = hbuf, hbufR
    if debug_out:
        debug_copy_phase(src, srcR)
    else:
        final_phase(src, srcR)
    P.add("sp", None, out_written, ())

    P.finalize()
    with nc.Block() as block:
        @block.tensor
        def _(e):
            P.emit("pe", e)

        @block.scalar
        def _(e):
            P.emit("act", e)

        @block.vector
        def _(e):
            P.emit("dve", e)

        @block.gpsimd
        def _(e):
            P.emit("pool", e)

        @block.sync
        def _(e):
            P.emit("sp", e)
    es.close()
    return nc, P


def _host_consts():
    c = np.zeros((128, CST_W), np.float32)
    c[:, C_ID:C_ID + 128] = np.eye(128, dtype=np.float32)
    p = np.arange(128)[:, None] % 64
    t = np.arange(64)[None, :]
    caus = (p <= t).astype(np.float32)
    c[:, C_CAUS:C_CAUS + 512] = np.tile(caus, (1, 8))
    s = np.arange(128)[:, None]
    tcol = np.arange(128)[None, :]
    for gi, w in enumerate(WINS):
        Dw = ((s <= tcol) & (s >= tcol - w + 1)).astype(np.float32) / w - np.eye(128, dtype=np.float32)
        Ow = ((s - 128) >= (tcol - w + 1)).astype(np.float32) / w
        O1 = np.zeros((128, 128), np.float32)
        O1[:64] = Ow[64:]
        Dpre = np.zeros((128, 128), np.float32)
        for tc in range(48, 64):
            cntv = min(tc - 48 + 1, w)
            for sr in range(max(0, tc - w + 1), tc + 1):
                Dpre[sr, tc] = 1.0 / cntv
            Dpre[tc, tc] -= 1.0
        for k, M in enumerate((Dw, Ow, O1, Dpre)):
            col = C_BAND + (k * 4 + gi) * 128
            c[:, col:col + 128] = M
    return c


def _host_pvec(inp):
    pvv = np.zeros((128, PV_W), np.float32)
    for j in range(2):
        b = PV_EV + j * PV_EV_W
        cw = inp["ev_conv_w"][j]
        pvv[:, b:b + 124] = cw.reshape(31, 4, 128).transpose(2, 1, 0).reshape(128, 124)
        pvv[:, b + 124:b + 128] = inp["ev_conv_b"][j].reshape(4, 128).T
        pvv[:, b + 128:b + 132] = inp["ev_ln_g"][j].reshape(4, 128).T
        pvv[:, b + 132:b + 136] = inp["ev_ln_b"][j].reshape(4, 128).T
        pvv[:, b + 136:b + 140] = inp["ev_pool_b"][j].T
        pvv[:, b + 140:b + 144] = inp["ev_pool_scale"][j].reshape(4, 128).T
        pvv[:, PV_OD + j] = inp["od_gnorm_g"][j]
    pvv[:, PV_LB:PV_LB + 32] = inp["lb_param"].reshape(4, 8, 128).transpose(2, 1, 0).reshape(128, 32)
    return pvv


def _in_maps(inp):
    inp = {k: np.ascontiguousarray(np.asarray(v, dtype=np.float32)) for k, v in inp.items()}
    cstv = _host_consts()
    pvv = _host_pvec(inp)
    gamv = np.concatenate([inp["mix_norm_g"], inp["mlp_norm_g"], inp["final_norm_g"][None]], 0)
    maps = []
    for b in range(BATCH):
        hin = np.zeros((TTOT, D), np.float32)
        hin[PRE - NMETA:PRE] = inp["meta_tokens"]
        hin[PRE:] = inp["x"][b]
        maps.append({
            "hin": hin, "gam": gamv, "cst": cstv, "pvec": pvv,
            "ev_w_in": inp["ev_w_in"], "ev_pool_w": inp["ev_pool_w"], "ev_w_out": inp["ev_w_out"],
            "od_w_in": inp["od_w_in"], "od_w_out": inp["od_w_out"],
            "mlp_w1": inp["mlp_w1"], "mlp_w2": inp["mlp_w2"],
        })
    return maps


def kernel(**inputs):
    nc, _ = build_program()
    maps = _in_maps(inputs)
    res = run_bass_kernel_spmd(nc, maps, core_ids=list(range(NCORES)))
    return np.stack([np.asarray(r["out"], dtype=np.float32) for r in res.results], axis=0)
```

```python
import numpy as np
from contextlib import ExitStack

import concourse.bass as bass
import concourse.mybir as mybir
from concourse.bass_utils import run_bass_kernel_spmd

F32 = mybir.dt.float32
BF16 = mybir.dt.bfloat16
I32 = mybir.dt.int32
AF = mybir.ActivationFunctionType
ALU = mybir.AluOpType

D = 1024
KC = 8
NMETA = 16
SEQ = 4096
BATCH = 4
EPS = 1e-6
TT = 256
PRE = 64
NCORES = 4
NT_FULL = SEQ // TT
TTOT = PRE + SEQ
WINS = (2, 4, 8, 16)

C_ID = 0
C_CAUS = 128
C_BAND = C_CAUS + 512
CST_W = C_BAND + 16 * 128

PV_EV = 0
PV_EV_W = 144
PV_OD = 2 * PV_EV_W
PV_LB = PV_OD + 2
PV_W = PV_LB + 32


class Res:
    __slots__ = ("name", "last_w", "rd_eng", "rd_dma", "sem", "ndma")
    fence = None

    def __init__(self, name, fenced=False):
        self.name = name
        self.last_w = Res.fence if fenced else None
        self.rd_eng = {}
        self.rd_dma = []
        self.sem = None
        self.ndma = 0


class Op:
    __slots__ = ("eng", "fn", "reads", "writes", "dma", "idx", "need", "tok", "waits", "inc",
                 "attach")


class Prog:
    ENGS = ("pe", "act", "dve", "pool", "sp")

    def __init__(self, nc, es):
        self.nc = nc
        self.es = es
        self.ops = []
        self.nsem = 0

    def add(self, eng, fn, reads=(), writes=(), dma=False, attach=None):
        op = Op()
        op.attach = (eng != "pe") if attach is None else attach
        op.eng = eng
        op.fn = fn
        op.reads = [r for r in reads if r is not None]
        op.writes = [r for r in writes if r is not None]
        op.dma = dma
        op.idx = len(self.ops)
        op.tok = None
        op.waits = []
        op.inc = False
        self.ops.append(op)
        return op

    def fence(self, res_list, fn):
        op = self.add("dve", fn, (), list(res_list))
        Res.fence = op.idx
        return op

    def _newsem(self, name):
        self.nsem += 1
        return self.es.enter_context(self.nc.semaphore(name))

    def finalize(self):
        ops = self.ops
        for op in ops:
            deps = {}

            def dep(i, hard):
                if i is not None and i != op.idx:
                    deps[i] = deps.get(i, False) or hard

            for r in op.reads:
                dep(r.last_w, True)
            for r in op.writes:
                dep(r.last_w, True)
                for i in r.rd_eng.values():
                    dep(i, False)
                for i in r.rd_dma:
                    dep(i, False)
            for r in op.reads:
                if op.dma:
                    r.rd_dma.append(op.idx)
                else:
                    r.rd_eng[op.eng] = op.idx
            for r in op.writes:
                r.last_w = op.idx
                r.rd_eng = {}
                r.rd_dma = []
            need = []
            for i, hard in deps.items():
                p = ops[i]
                if p.eng == op.eng and not p.dma and not op.dma:
                    if op.eng == "pe":
                        continue
                need.append(i)
                p.inc = True
            need.sort()
            op.need = need
        engsem = {e: self._newsem("s_" + e) for e in ("pe", "act", "dve", "pool")}
        cnt = {e: 0 for e in engsem}
        for op in ops:
            if op.dma:
                r = op.writes[0]
                if r.sem is None:
                    r.sem = self._newsem("d_" + r.name)
                r.ndma += 1
                op.tok = (r.sem, 16 * r.ndma)
            elif op.inc:
                cnt[op.eng] += 1
                op.tok = (engsem[op.eng], cnt[op.eng])
        known = {e: {} for e in self.ENGS}
        frozen = {e: None for e in self.ENGS}
        snap = {}
        for op in ops:
            k = known[op.eng]
            for i in op.need:
                p = ops[i]
                s, v = p.tok
                sid = id(s)
                if k.get(sid, (None, 0))[1] >= v:
                    continue
                op.waits.append((s, v))
                base, own = snap[i]
                for sid2, sv in base.items():
                    if k.get(sid2, (None, 0))[1] < sv[1]:
                        k[sid2] = sv
                k[sid] = (s, v)
                frozen[op.eng] = None
            if op.tok is not None:
                if frozen[op.eng] is None:
                    frozen[op.eng] = dict(k)
                snap[op.idx] = (frozen[op.eng], op.tok)
        self.stats = {e: sum(1 for o in ops if o.eng == e) for e in self.ENGS}
        self.nwaits = {e: sum(len(o.waits) for o in ops if o.eng == e) for e in self.ENGS}

    def emit(self, eng, e):
        for op in self.ops:
            if op.eng != eng:
                continue
            waits = op.waits
            if op.fn is None:
                for s, v in waits:
                    e.wait_ge(s, v)
                continue
            if op.attach and waits:
                for s, v in waits[:-1]:
                    e.wait_ge(s, v)
                ins = op.fn(e)
                ins._wait_ge(waits[-1][0], waits[-1][1])
            else:
                for s, v in waits:
                    e.wait_ge(s, v)
                ins = op.fn(e)
            if op.tok is not None:
                ins.then_inc(op.tok[0], 16 if op.dma else 1)


class Buf:
    def __init__(self, t, name):
        self.t = t
        self.r = Res(name)

    def __getitem__(self, key):
        return self.t[key]


class Ring:
    def __init__(self, bufs):
        self.bufs = bufs
        self.i = 0

    def next(self):
        b = self.bufs[self.i % len(self.bufs)]
        self.i += 1
        return b


def build_program(phase_list=None, debug_out=False):
    nc = bass.Bass("TRN2", target_bir_lowering=False)
    es = ExitStack()
    P = Prog(nc, es)
    Res.fence = None

    def din(name, shape):
        return nc.dram_tensor(name, list(shape), F32, kind="ExternalInput").ap()

    hin = din("hin", [TTOT, D])
    gam = din("gam", [9, D])
    cst = din("cst", [128, CST_W])
    pvec = din("pvec", [128, PV_W])
    ev_w_in = din("ev_w_in", [2, D, 1536])
    ev_pool_w = din("ev_pool_w", [2, 4, 128, 128])
    ev_w_out = din("ev_w_out", [2, D, D])
    od_w_in = din("od_w_in", [2, D, 4096])
    od_w_out = din("od_w_out", [2, D, D])
    mlp_w1 = din("mlp_w1", [4, D, 4096])
    mlp_w2 = din("mlp_w2", [4, 4096, D])
    if debug_out:
        out = nc.dram_tensor("out", [TTOT, D], F32, kind="ExternalOutput").ap()
    else:
        out = nc.dram_tensor("out", [SEQ, D], F32, kind="ExternalOutput").ap()
    hbuf = nc.dram_tensor("hbuf", [TTOT, D], F32, kind="Internal").ap()

    ntiles = 1 + NT_FULL
    hbufR = [Res("hb%d" % j) for j in range(ntiles)]
    outR = [Res("ob%d" % j) for j in range(ntiles)]
    out_written = []

    def tinfo(j):
        if j == 0:
            return 0, PRE, 64, 1
        return PRE + (j - 1) * TT, TT, 128, TT // 128

    def sb(name, shape, dt):
        return Buf(es.enter_context(nc.sbuf_tensor(name, list(shape), dt)), name)

    U = [sb("U%d" % i, [128, 4096], BF16) for i in range(16)]
    hT = Ring([sb("hT%d" % i, [128, 2, D], F32) for i in range(2)])
    nT = Ring([sb("nT%d" % i, [128, KC, TT], BF16) for i in range(2)])
    ntok = sb("ntok", [128, 2, D], BF16)
    gbc = sb("gbc", [128, D], F32)
    band = sb("band", [128, 2048], BF16)
    causm = sb("causm", [128, 512], mybir.dt.uint8)
    pv = sb("pv", [128, PV_W], F32)
    ident = sb("ident", [128, 128], BF16)
    identh = sb("identh", [128, 128], BF16)
    ones32 = sb("ones32", [128, 128], F32)
    hmask = sb("hmask", [128, 4], F32)
    st = Ring([sb("st%d" % i, [128, 8], F32) for i in range(2)])
    psum = [Buf(es.enter_context(nc.psum_tensor("ps%d" % i, [128, 512], F32)), "ps%d" % i)
            for i in range(8)]
    G = [sb("G%d" % i, [128, 2048], BF16) for i in range(8)]
    Sst = sb("Sst", [128, 8, 128], F32)
    Spb = sb("Spb", [128, 8, 128], BF16)
    small = sb("small", [128, 128], F32)
    fz = sb("fz", [128, 1], F32)
    rbufs = [sb("rb%d" % i, [128, TT], BF16) for i in range(2)]

    def v3(ap, a):
        return ap.rearrange("p (a b) -> p a b", a=a)

    phase_scratch = []

    def carve(g, off, nbytes, dt, name):
        ap = G[g].t[:, off // 2:(off + nbytes) // 2]
        if dt != BF16:
            ap = ap.bitcast(dt)
        b = Buf.__new__(Buf)
        b.t = ap
        b.r = Res(name, fenced=True)
        phase_scratch.append(b.r)
        return b

    def end_phase():
        rl = list(phase_scratch)
        del phase_scratch[:]
        P.fence(rl + [fz.r], lambda e: e.memset(fz[:, :], 0.0))

    def mm(out_ap, lhsT, rhs, start, stop, R, W, sgc=False):
        if sgc:
            P.add("pe", lambda e: e.matmul(out_ap, lhsT, rhs, start=start, stop=stop,
                                           skip_group_check=True), R, W)
        else:
            P.add("pe", lambda e: e.matmul(out_ap, lhsT, rhs, start=start, stop=stop), R, W)

    def tr(out_ap, in_ap, idn, R, W):
        P.add("pe", lambda e: e.transpose(out_ap, in_ap, idn), R, W)

    def act(out_ap, in_ap, func, R, W, bias=None, scale=None, accum=None):
        kw = {}
        if bias is not None:
            kw["bias"] = bias
        if scale is not None:
            kw["scale"] = scale
        if accum is not None:
            kw["accum_out"] = accum
        P.add("act", lambda e: e.activation(out_ap, in_ap, func, **kw), R, W,
              attach=(accum is None))

    def tt(eng, out_ap, a, b, op, R, W):
        P.add(eng, lambda e: e.tensor_tensor(out_ap, a, b, op), R, W)

    def ts(eng, out_ap, a, s1, s2, op0, op1, R, W):
        if op1 is None:
            P.add(eng, lambda e: e.tensor_scalar(out_ap, a, s1, None, op0), R, W)
        else:
            P.add(eng, lambda e: e.tensor_scalar(out_ap, a, s1, s2, op0, op1), R, W)

    def stt(out_ap, a, sc, b, op0, op1, R, W):
        P.add("dve", lambda e: e.scalar_tensor_tensor(out_ap, a, sc, b, op0, op1), R, W)

    def cp(eng, out_ap, in_ap, R, W):
        if eng == "act":
            P.add("act", lambda e: e.copy(out_ap, in_ap), R, W)
        else:
            P.add(eng, lambda e: e.tensor_copy(out_ap, in_ap), R, W)

    def memset(eng, ap, val, W):
        P.add(eng, lambda e: e.memset(ap, val), (), W)

    def recip(out_ap, in_ap, R, W):
        P.add("dve", lambda e: e.reciprocal(out_ap, in_ap), R, W)

    def dma(q, out_ap, in_ap, R, Wres):
        P.add(q, lambda e: e.dma_start(out=out_ap, in_=in_ap), R, [Wres], dma=True)

    pA = Ring(psum[0:3])
    pB = Ring(psum[3:6])
    pT = Ring(psum[6:8])

    c0 = G[0][:, 0:2 * C_BAND].bitcast(F32)
    dma("sp", c0[:, :], cst[:, 0:C_BAND], [], G[0].r)
    dma("sp", pv[:, :], pvec[:, :], [], pv.r)
    cp("dve", ident[:, :], c0[:, C_ID:C_ID + 128], [G[0].r], [ident.r])
    ts("dve", identh[:, :], c0[:, C_ID:C_ID + 128], 0.5, None, ALU.mult, None, [G[0].r],
       [identh.r])
    cp("dve", causm[:, :], c0[:, C_CAUS:C_CAUS + 512], [G[0].r], [causm.r])
    for i in range(2):
        ci = G[1 + i][:, :].bitcast(F32)
        dma("sp", ci[:, :], cst[:, C_BAND + 1024 * i:C_BAND + 1024 * (i + 1)], [], G[1 + i].r)
        cp("dve", band[:, 1024 * i:1024 * (i + 1)], ci[:, :], [G[1 + i].r], [band.r])
    memset("dve", ones32[:, :], 1.0, [ones32.r])
    memset("dve", hmask[:, :], 1.0, [hmask.r])
    memset("dve", hmask[0:48, 1:2], 0.0, [hmask.r])
    memset("dve", hmask[:, 2:3], EPS, [hmask.r])
    epsc = hmask[:, 2:3]
    P.fence([G[0].r, G[1].r, G[2].r, fz.r], lambda e: e.memset(fz[:, :], 0.0))

    def load_tile(j, src, srcR):
        t0, tw, rows, nsub = tinfo(j)
        h = hT.next()
        dma("sp", h[:rows, :nsub, :], src[t0:t0 + tw, :].rearrange("(s p) d -> p s d", p=rows),
            [srcR[j]] if srcR is not None else [], h.r)
        return h

    def tile_rstd(j, h):
        t0, tw, rows, nsub = tinfo(j)
        s = st.next()
        for q in range(nsub):
            act(ntok[:rows, q, :], h[:rows, q, :], AF.Square, [h.r], [ntok.r, s.r],
                accum=s[:rows, q:q + 1])
        act(s[:rows, 4:4 + nsub], s[:rows, 0:nsub], AF.Ln, [s.r, hmask.r], [s.r],
            bias=epsc[:rows, :], scale=1.0 / D)
        act(s[:rows, 6:6 + nsub], s[:rows, 4:4 + nsub], AF.Exp, [s.r], [s.r], scale=-0.5)
        return s

    def head_a(j, src, srcR):
        t0, tw, rows, nsub = tinfo(j)
        h = load_tile(j, src, srcR)
        s = tile_rstd(j, h)
        for q in range(nsub):
            stt(ntok[:rows, q, :], h[:rows, q, :], s[:rows, 6 + q:7 + q], gbc[:rows, :],
                ALU.mult, ALU.mult, [h.r, s.r, gbc.r], [ntok.r])
        return h

    def head_b(j, h):
        t0, tw, rows, nsub = tinfo(j)
        n = nT.next()
        for q in range(nsub):
            ps = pT.next()
            psb = v3(ps[:, :].bitcast(BF16), 8)
            for c in range(KC):
                tr(psb[:, c, :rows], ntok[:rows, q, c * 128:(c + 1) * 128], ident[:rows, :rows],
                   [ntok.r, ident.r], [ps.r])
            cp("act", n[:, :, q * 128:q * 128 + rows], psb[:, :, :rows], [ps.r], [n.r])
        return h, n

    def head(j, src, srcR):
        return head_b(j, head_a(j, src, srcR))

    def finish_tile(j, h, dst, dstR):
        t0, tw, rows, nsub = tinfo(j)
        dma("sp", dst[t0:t0 + tw, :].rearrange("(s p) d -> p s d", p=rows), h[:rows, :nsub, :],
            [h.r], dstR[j])

    def resid_add(j, h, q, half, ps):
        t0, tw, rows, nsub = tinfo(j)
        mcol = 1 if j == 0 else 0
        stt(h[:rows, q, half * 512:(half + 1) * 512], ps[:rows, :], hmask[:rows, mcol:mcol + 1],
            h[:rows, q, half * 512:(half + 1) * 512], ALU.mult, ALU.add,
            [ps.r, hmask.r, h.r], [h.r])

    def load_gamma(row):
        dma("sp", gbc[:, :], gam[row:row + 1, :].broadcast_to([128, D]), [], gbc.r)

    def pipeline(src, srcR, stageB):
        heads = {0: head(0, src, srcR)}

        for j in range(ntiles):
            part = {}

            def mid(stage=1, j=j, part=part):
                if j + 1 >= ntiles:
                    return None
                if (j + 1) in heads:
                    return heads[j + 1]
                if "a" not in part:
                    part["a"] = head_a(j + 1, src, srcR)
                if stage == 0:
                    return None
                heads[j + 1] = head_b(j + 1, part["a"])
                return heads[j + 1]
            stageB(j, heads[j][0], heads[j][1], mid)
            mid(1)
            del heads[j]

    def mlp_phase(L, src, srcR, dst, dstR):
        W1v = [v3(U[u][:, :], 8) for u in range(8)]
        W2v = [v3(U[8 + u][:, :], 4) for u in range(8)]
        load_gamma(4 + L)
        for u in range(8):
            dma("pool", W1v[u], mlp_w1[L, :, u * 512:(u + 1) * 512].rearrange(
                "(kc p) n -> p kc n", p=128), [], U[u].r)
        for u in range(8):
            dma("pool", W2v[u], mlp_w2[L, u * 512:(u + 1) * 512, :].rearrange(
                "(hc p) n -> p hc n", p=128), [], U[8 + u].r)
        hidb = [carve(i, 0, 4096, BF16, "hid%d" % i) for i in range(4)]
        hidv = [v3(b.t, 8) for b in hidb]
        hidR = [b.r for b in hidb]
        rr = Ring([(b[:, :], b.r) for b in rbufs])

        def stageB(j, h, n, mid):
            t0, tw, rows, nsub = tinfo(j)
            for hc in range(32):
                if hc == 8:
                    mid(0)
                if hc == 18:
                    mid(1)
                ps = pA.next()
                for kc in range(KC):
                    mm(ps[:, :tw], W1v[hc // 4][:, kc, (hc % 4) * 128:(hc % 4 + 1) * 128],
                       n[:, kc, :tw], kc == 0, kc == KC - 1, [U[hc // 4].r, n.r], [ps.r])
                r_ap, r_res = rr.next()
                act(r_ap[:, :tw], ps[:, :tw], AF.Relu, [ps.r], [r_res])
                tt("pool", hidv[hc // 8][:, hc % 8, :tw], r_ap[:, :tw], r_ap[:, :tw], ALU.mult,
                   [r_res], [hidR[hc // 8]])
            for q in range(nsub):
                for half in range(2):
                    ps = pB.next()
                    for hc in range(32):
                        mm(ps[:rows, :], hidv[hc // 8][:, hc % 8, q * 128:q * 128 + rows],
                           W2v[hc // 4][:, hc % 4, half * 512:(half + 1) * 512],
                           hc == 0, hc == 31, [hidR[hc // 8], U[8 + hc // 4].r], [ps.r])
                    resid_add(j, h, q, half, ps)
            finish_tile(j, h, dst, dstR)

        pipeline(src, srcR, stageB)
        end_phase()

    def lb_setup():
        lbp = pv[:, PV_LB:PV_LB + 32].rearrange("p (h l) -> p h l", l=4)
        tmp = small[:, 56:64]
        e = U[13][:, 0:64].bitcast(F32).rearrange("p (h l) -> p h l", l=4)
        P.add("dve", lambda en: en.tensor_reduce(tmp, lbp, mybir.AxisListType.X, ALU.max),
              [pv.r], [small.r])
        tt("dve", e, lbp, small[:, 56:64].unsqueeze(2).broadcast_to([128, 8, 4]), ALU.subtract,
           [pv.r, small.r], [U[13].r])
        act(e, e, AF.Exp, [U[13].r], [U[13].r])
        P.add("dve", lambda en: en.tensor_reduce(tmp, e, mybir.AxisListType.X, ALU.add),
              [U[13].r], [small.r])
        recip(tmp, tmp, [small.r], [small.r])
        tt("dve", e, e, small[:, 56:64].unsqueeze(2).broadcast_to([128, 8, 4]), ALU.mult,
           [U[13].r, small.r], [U[13].r])
        cp("dve", small[:, 0:8], e[:, :, 1], [U[13].r], [small.r])
        tt("dve", small[:, 24:32], e[:, :, 1], e[:, :, 2], ALU.add, [U[13].r], [small.r])
        tt("dve", small[:, 24:32], small[:, 24:32], e[:, :, 3], ALU.add, [U[13].r, small.r],
           [small.r])
        for base in (0, 24):
            ts("dve", small[:, base + 8:base + 16], small[:, base:base + 8], -0.5, 0.5,
               ALU.mult, ALU.add, [small.r], [small.r])
            ts("dve", small[:, base + 16:base + 24], small[:, base:base + 8], 0.5, -0.5,
               ALU.mult, ALU.add, [small.r], [small.r])
            ts("dve", small[:, base:base + 8], small[:, base:base + 8], 0.5, 0.5,
               ALU.mult, ALU.add, [small.r], [small.r])

    def even_phase(L, src, srcR, dst, dstR):
        jl = L // 2
        pb = PV_EV + jl * PV_EV_W
        Win = [v3(U[u][:, :], 8) for u in range(3)]
        Wout = [v3(U[8 + u][:, :], 4) for u in range(2)]
        diag = [v3(U[10 + g][:, :], 32) for g in range(4)]
        poolw = v3(U[3][:, 0:512], 4)
        load_gamma(L)
        for u in range(3):
            dma("pool", Win[u], ev_w_in[jl, :, u * 512:(u + 1) * 512].rearrange(
                "(kc p) n -> p kc n", p=128), [], U[u].r)
        dma("pool", poolw, ev_pool_w[jl].rearrange("g c d -> c g d"), [], U[3].r)
        for u in range(2):
            dma("pool", Wout[u], ev_w_out[jl, u * 512:(u + 1) * 512, :].rearrange(
                "(kc p) n -> p kc n", p=128), [], U[8 + u].r)
        for g in range(4):
            c = pb + g * 31
            tt("dve", diag[g][:, 0:31, :], identh[:, :].unsqueeze(1).broadcast_to([128, 31, 128]),
               pv[:, c:c + 31].unsqueeze(2).broadcast_to([128, 31, 128]), ALU.mult,
               [identh.r, pv.r], [U[10 + g].r])
        pbs = small[:, 48 + 4 * jl:52 + 4 * jl]
        tt("dve", pbs, pv[:, pb + 136:pb + 140], pv[:, pb + 140:pb + 144], ALU.mult,
           [pv.r], [small.r])
        cb = lambda g: pv[:, pb + 124 + g:pb + 125 + g]
        lng = lambda g: pv[:, pb + 128 + g:pb + 129 + g]
        lnb = lambda g: pv[:, pb + 132 + g:pb + 133 + g]
        psc = lambda g: pv[:, pb + 140 + g:pb + 141 + g]

        AW = 32 + TT
        aT = [carve(i, 0, 4 * AW * 2, BF16, "aT%d" % i) for i in range(2)]
        aTv = [v3(b.t, 4) for b in aT]
        sig = Ring([carve(i, 2560, 1024, F32, "sig%d" % i) for i in range(2)])
        yb_ = carve(2, 0, 4096, F32, "y")
        ysq = carve(3, 0, 4096, F32, "ysq")
        yv, ysqv = v3(yb_.t, 4), v3(ysq.t, 4)
        mean = carve(4, 0, 1024, F32, "mean")
        msq = carve(4, 1024, 1024, F32, "msq")
        var = carve(4, 2048, 1024, F32, "var")
        rstd = carve(4, 3072, 1024, F32, "rstd")
        zz = Ring([carve(5, i * 1024, 1024, F32, "z%d" % i) for i in range(2)])
        pdT = carve(5, 2048, 2048, BF16, "pdT")
        pdTv = v3(pdT.t, 4)
        yT = carve(6, 0, 4096, BF16, "yT")
        yTv = v3(yT.t, 8)
        pinc = carve(7, 0, 2048, BF16, "pinc")
        pincv = v3(pinc.t, 2)
        pinp = carve(7, 2048, 1024, BF16, "pinp")
        memset("dve", aT[1].t[:, :], 0.0, [aT[1].r])

        def bandm(kind, gi):
            c = (kind * 4 + gi) * 128
            return band[:, c:c + 128]

        def stageB(j, h, n, mid):
            t0, tw, rows, nsub = tinfo(j)
            cur, prv = aTv[j % 2], aTv[(j + 1) % 2]
            curR, prvR = aT[j % 2].r, aT[(j + 1) % 2].r
            twp = tinfo(j - 1)[1] if j > 0 else TT
            cp("pool", cur[:, :, 0:32], prv[:, :, twp:twp + 32], [prvR], [curR])
            for g in range(4):
                psv = pA.next()
                for kc in range(KC):
                    mm(psv[:, :tw], Win[0][:, kc, g * 128:(g + 1) * 128], n[:, kc, :tw],
                       kc == 0, kc == KC - 1, [U[0].r, n.r], [psv.r])
                psg = pA.next()
                for kc in range(KC):
                    mm(psg[:, :tw], Win[1][:, kc, g * 128:(g + 1) * 128], n[:, kc, :tw],
                       kc == 0, kc == KC - 1, [U[1].r, n.r], [psg.r])
                sg_ = sig.next()
                act(sg_.t[:, :tw], psg[:, :tw], AF.Tanh, [psg.r], [sg_.r], scale=0.5)
                stt(cur[:, g, 32:32 + tw], sg_.t[:, :tw], 1.0, psv[:, :tw], ALU.add, ALU.mult,
                    [psv.r, sg_.r], [curR])
            mid(0)
            for q in range(nsub):
                ps = pB.next()
                for kc in range(KC):
                    mm(ps[:rows, :], n[:, kc, q * 128:q * 128 + rows], Win[2][:, kc, :],
                       kc == 0, kc == KC - 1, [n.r, U[2].r], [ps.r])
                cp("act", pincv[:rows, q, :], ps[:rows, :], [ps.r], [pinc.r])
            for g in range(4):
                if g == 2:
                    mid(1)
                psc_ = pA.next()
                for tap in range(31):
                    mm(psc_[:, :tw], diag[g][:, tap, :], cur[:, g, 2 + tap:2 + tap + tw],
                       tap == 0, tap == 30, [U[10 + g].r, curR], [psc_.r])
                act(yv[:, g, :tw], psc_[:, :tw], AF.Identity, [psc_.r, pv.r], [yb_.r], bias=cb(g))
                act(ysqv[:, g, :tw], psc_[:, :tw], AF.Square, [psc_.r, pv.r], [ysq.r], bias=cb(g))
            ps1 = pB.next()
            for g in range(4):
                mm(ps1[:, :tw], ones32[:, :], yv[:, g, :tw], g == 0, g == 3, [ones32.r, yb_.r],
                   [ps1.r])
            ps2 = pB.next()
            for g in range(4):
                mm(ps2[:, :tw], ones32[:, :], ysqv[:, g, :tw], g == 0, g == 3, [ones32.r, ysq.r],
                   [ps2.r])
            ts("dve", mean.t[:, :tw], ps1[:, :tw], 1.0 / 512, None, ALU.mult, None, [ps1.r],
               [mean.r])
            tt("dve", msq.t[:, :tw], mean.t[:, :tw], mean.t[:, :tw], ALU.mult, [mean.r], [msq.r])
            ts("dve", msq.t[:, :tw], msq.t[:, :tw], -1.0, EPS, ALU.mult, ALU.add, [msq.r], [msq.r])
            stt(var.t[:, :tw], ps2[:, :tw], 1.0 / 512, msq.t[:, :tw], ALU.mult, ALU.add,
                [ps2.r, msq.r], [var.r])
            act(var.t[:, :tw], var.t[:, :tw], AF.Ln, [var.r], [var.r])
            act(rstd.t[:, :tw], var.t[:, :tw], AF.Exp, [var.r], [rstd.r], scale=-0.5)
            for g in range(4):
                z = zz.next()
                tt("dve", z.t[:, :tw], yv[:, g, :tw], mean.t[:, :tw], ALU.subtract,
                   [yb_.r, mean.r], [z.r])
                tt("dve", z.t[:, :tw], z.t[:, :tw], rstd.t[:, :tw], ALU.mult, [z.r, rstd.r], [z.r])
                act(yTv[:, g, :tw], z.t[:, :tw], AF.Silu, [z.r, pv.r], [yT.r], bias=lnb(g),
                    scale=lng(g))
            for gi in range(4):
                ps = pB.next()
                for q in range(nsub):
                    oc = ps[:, q * 128:q * 128 + rows]
                    have_prev = not (j == 0 and q == 0)
                    if have_prev:
                        if q > 0:
                            mm(oc, pincv[:, q - 1, gi * 128:(gi + 1) * 128], bandm(1, gi)[:, :rows],
                               True, False, [pinc.r, band.r], [ps.r])
                        elif j == 1:
                            mm(oc, pinp.t[:64, gi * 128:(gi + 1) * 128], bandm(2, gi)[:64, :rows],
                               True, False, [pinp.r, band.r], [ps.r])
                        else:
                            mm(oc, pinp.t[:, gi * 128:(gi + 1) * 128], bandm(1, gi)[:, :rows],
                               True, False, [pinp.r, band.r], [ps.r])
                    dk = 3 if j == 0 else 0
                    mm(oc, pincv[:rows, q, gi * 128:(gi + 1) * 128], bandm(dk, gi)[:rows, :rows],
                       not have_prev, True, [pinc.r, band.r], [ps.r])
                cp("act", pdTv[:, gi, :tw], ps[:, :tw], [ps.r], [pdT.r])
                psl = pB.next()
                mm(psl[:, :tw], poolw[:, gi, :], pdTv[:, gi, :tw], True, True, [U[3].r, pdT.r],
                   [psl.r])
                act(yTv[:, 4 + gi, :tw], psl[:, :tw], AF.Identity, [psl.r, pv.r, small.r], [yT.r],
                    bias=pbs[:, gi:gi + 1], scale=psc(gi))
            cp("pool", pinp.t[:rows, :], pincv[:rows, nsub - 1, :], [pinc.r], [pinp.r])
            for q in range(nsub):
                for half in range(2):
                    ps = pB.next()
                    for kc in range(KC):
                        mm(ps[:rows, :], yTv[:, kc, q * 128:q * 128 + rows],
                           Wout[kc // 4][:, kc % 4, half * 512:(half + 1) * 512],
                           kc == 0, kc == KC - 1, [yT.r, U[8 + kc // 4].r], [ps.r])
                    resid_add(j, h, q, half, ps)
            finish_tile(j, h, dst, dstR)

        pipeline(src, srcR, stageB)
        end_phase()

    def hgrn_phase(L, src, srcR, dst, dstR):
        jl = L // 2
        lbb = 0 if L == 1 else 24
        lbc = lambda h: small[:, lbb + h:lbb + h + 1]
        omlc = lambda h: small[:, lbb + 8 + h:lbb + 9 + h]
        nomlc = lambda h: small[:, lbb + 16 + h:lbb + 17 + h]
        gng = pv[:, PV_OD + jl:PV_OD + jl + 1]
        Win = [v3(U[u][:, :], 8) for u in range(8)]
        Wout = [v3(U[8 + u][:, :], 4) for u in range(2)]
        load_gamma(L)
        for u in (0, 1, 6, 7, 2, 3, 4, 5):
            dma("pool", Win[u], od_w_in[jl, :, u * 512:(u + 1) * 512].rearrange(
                "(kc p) n -> p kc n", p=128), [], U[u].r)
        for u in range(2):
            dma("pool", Wout[u], od_w_out[jl, u * 512:(u + 1) * 512, :].rearrange(
                "(kc p) n -> p kc n", p=128), [], U[8 + u].r)
        X = [v3(U[10 + i][:, :].bitcast(F32), 8) for i in range(5)]
        XR = [U[10 + i].r for i in range(5)]
        qs, sgm, lgf, bb, kk = X
        qsR, sgmR, lgfR, bbR, kkR = XR
        kdT = v3(U[15][:, 0:2048], 8)
        cmask = U[15][:, 2048:2560].bitcast(F32)
        kdTR = U[15].r
        sg = carve(0, 0, 4096, BF16, "sg")
        qt = carve(1, 0, 4096, BF16, "qt")
        kt = carve(2, 0, 4096, BF16, "kt")
        kdtm = carve(3, 0, 4096, BF16, "kdtm")
        vtm = carve(4, 0, 4096, BF16, "vtm")
        yT = carve(5, 0, 4096, BF16, "yT")
        scT = carve(6, 0, 2048, BF16, "scT")
        osq = carve(6, 2048, 2048, F32, "osq")
        rstd = carve(7, 0, 2048, F32, "rstd")
        t1 = carve(7, 2048, 2048, F32, "t1")
        sgv, qtv, ktv, yTv = v3(sg.t, 8), v3(qt.t, 8), v3(kt.t, 8), v3(yT.t, 8)
        kdtmv, vtmv, scTv = v3(kdtm.t, 2), v3(vtm.t, 2), v3(scT.t, 2)
        memset("dve", cmask[:, :], 1.0, [kdTR])
        for c in range(TT // 64):
            memset("dve", cmask[:, c * 64:c * 64 + 1], 0.0, [kdTR])
        memset("dve", scT.t[:, :], 0.0, [scT.r])
        memset("dve", Sst[:, :, :], 0.0, [Sst.r])
        emid_all = small[:, 64:96]
        elast_all = small[:, 96:128]
        early = set()
        pO = Ring([psum[0], psum[1]])
        pP = Ring([psum[2], psum[6]])

        def stageB(j, h, n, mid):
            t0, tw, rows, nsub = tinfo(j)
            nch = tw // 64
            nb = 8 * nch

            def proj(h_, ubase, func, dst_ap, dstR):
                ps = pP.next()
                for kc in range(KC):
                    mm(ps[:, :tw], Win[ubase + h_ // 4][:, kc, (h_ % 4) * 128:(h_ % 4 + 1) * 128],
                       n[:, kc, :tw], kc == 0, kc == KC - 1, [U[ubase + h_ // 4].r, n.r], [ps.r])
                act(dst_ap, ps[:, :tw], func, [ps.r], [dstR])

            def cv(x):
                if nch == 1:
                    return x[:, :, 0:64]
                return x[:, :, :].rearrange("p h (c t) -> p (h c) t", t=64)

            if j in early:
                act(sgm[:, :, :tw], sgm[:, :, :tw], AF.Tanh, [sgmR], [sgmR], scale=0.5)
                act(qs[:, :, :tw], qs[:, :, :tw], AF.Silu, [qsR], [qsR])
            else:
                for h_ in range(8):
                    ps = pP.next()
                    for kc in range(KC):
                        mm(ps[:, :tw], Win[2 + h_ // 4][:, kc, (h_ % 4) * 128:(h_ % 4 + 1) * 128],
                           n[:, kc, :tw], kc == 0, kc == KC - 1, [U[2 + h_ // 4].r, n.r], [ps.r])
                    act(sgm[:, h_, :tw], ps[:, :tw], AF.Tanh, [ps.r], [sgmR], scale=0.5)
                for h_ in range(8):
                    proj(h_, 0, AF.Silu, qs[:, h_, :tw], qsR)
            for h_ in range(8):
                proj(h_, 6, AF.Silu, sgv[:, h_, :tw], sg.r)
            mid(0)
            for h_ in range(8):
                act(lgf[:, h_, :tw], sgm[:, h_, :tw], AF.Ln, [sgmR, small.r], [lgfR],
                    bias=lbc(h_), scale=omlc(h_))
                ts("pool", kk[:, h_, :tw], sgm[:, h_, :tw], nomlc(h_), omlc(h_), ALU.mult, ALU.add,
                   [sgmR, small.r], [kkR])
                P.add("dve", (lambda en, h_=h_: en.tensor_tensor_scan(
                    bb[:, h_, :tw], cmask[:, :tw], lgf[:, h_, :tw], 0.0, ALU.mult, ALU.add)),
                    [kdTR, lgfR], [bbR])
            nxt = mid()
            early_jobs = []
            if nxt is not None:
                tw1 = tinfo(j + 1)[1]
                n1 = nxt[1]
                for h_ in range(8):
                    early_jobs.append((2, h_, sgm, sgmR))
                    early_jobs.append((0, h_, qs, qsR))
                early.add(j + 1)

            def run_early(k):
                for _ in range(k):
                    if not early_jobs:
                        return
                    ub, h_, dstb, dstR_ = early_jobs.pop(0)
                    ps = pP.next()
                    for kc in range(KC):
                        mm(ps[:, :tw1], Win[ub + h_ // 4][:, kc, (h_ % 4) * 128:(h_ % 4 + 1) * 128],
                           n1[:, kc, :tw1], kc == 0, kc == KC - 1, [U[ub + h_ // 4].r, n1.r], [ps.r])
                    act(dstb[:, h_, :tw1], ps[:, :tw1], AF.Identity, [ps.r], [dstR_])
            bv = cv(bb)
            bmid = bv[:, :, 31:32].broadcast_to([128, nb, 64])
            blast = bv[:, :, 63:64].broadcast_to([128, nb, 64])
            emid = emid_all[:, 0:nb]
            elast = elast_all[:, 0:nb]
            for q in range(nsub):
                for half in range(2):
                    ps = pB.next()
                    for kc in range(KC):
                        mm(ps[:rows, :], n[:, kc, q * 128:q * 128 + rows], Win[4 + half][:, kc, :],
                           kc == 0, kc == KC - 1, [n.r, U[4 + half].r], [ps.r])
                    cp("act", vtmv[:rows, q, half * 512:(half + 1) * 512], ps[:rows, :], [ps.r],
                       [vtm.r])
            tt("dve", cv(sgm), bv, bmid, ALU.subtract, [bbR, sgmR], [sgmR])
            tt("dve", cv(lgf), blast, bv, ALU.subtract, [bbR, lgfR], [lgfR])
            act(emid.unsqueeze(2), bv[:, :, 31:32], AF.Exp, [bbR], [small.r])
            act(elast.unsqueeze(2), bv[:, :, 63:64], AF.Exp, [bbR], [small.r])
            act(bv, cv(sgm), AF.Exp, [sgmR, bbR], [bbR])
            act(cv(sgm), cv(sgm), AF.Exp, [sgmR], [sgmR], scale=-1.0)
            act(cv(lgf), cv(lgf), AF.Exp, [lgfR], [lgfR])
            tt("dve", cv(ktv), cv(kk), cv(sgm), ALU.mult, [kkR, sgmR], [kt.r])
            tt("dve", cv(kdT), cv(kk), cv(lgf), ALU.mult, [kkR, lgfR], [kdTR])
            tt("dve", cv(qtv), cv(qs), bv, ALU.mult, [qsR, bbR], [qt.r])
            for q in range(nsub):
                ps = pT.next()
                psb = v3(ps[:, :].bitcast(BF16), 8)
                for h_ in range(8):
                    tr(psb[:rows, h_, :], kdT[:, h_, q * 128:q * 128 + rows], ident[:, :],
                       [kdTR, ident.r], [ps.r])
                cp("act", kdtmv[:rows, q, :], psb[:rows, :, :].rearrange("p h k -> p (h k)"),
                   [ps.r], [kdtm.r])
            for q in range(nsub):
                pa_ = pB.next()
                for h_ in range(8):
                    mm(pa_[0:64, h_ * 64:(h_ + 1) * 64], ktv[:, h_, q * 128:q * 128 + 64],
                       qtv[:, h_, q * 128:q * 128 + 64], True, True, [kt.r, qt.r], [pa_.r])
                P.add("dve", (lambda en, pa_=pa_, q=q: en.copy_predicated(
                    scTv[0:64, q, :], causm[0:64, :], pa_[0:64, :])), [pa_.r, causm.r], [scT.r])
                if rows == 128:
                    pb_ = pB.next()
                    for h_ in range(8):
                        mm(pb_[:, h_ * 64:(h_ + 1) * 64], ktv[:, h_, q * 128:q * 128 + 128],
                           qtv[:, h_, q * 128 + 64:q * 128 + 128], True, True, [kt.r, qt.r],
                           [pb_.r])
                    P.add("dve", (lambda en, pb_=pb_, q=q: en.copy_predicated(
                        scTv[64:128, q, :], causm[64:128, :], pb_[64:128, :])),
                        [pb_.r, causm.r], [scT.r])

            def gnorm(c, pso):
                def run():
                    act(osq.t[:, :], pso[:, :], AF.Square, [pso.r], [osq.r])
                    psm = pB.next()
                    mm(psm[:, :], ones32[:, :], osq.t[:, :], True, True, [ones32.r, osq.r],
                       [psm.r])
                    act(rstd.t[:, :], psm[:, :], AF.Ln, [psm.r, hmask.r], [rstd.r], bias=epsc,
                        scale=1.0 / 128)
                    act(rstd.t[:, :], rstd.t[:, :], AF.Exp, [rstd.r], [rstd.r], scale=-0.5)
                    act(t1.t[:, :], pso[:, :], AF.Identity, [pso.r, pv.r], [t1.r], scale=gng)
                    tt("pool", t1.t[:, :], t1.t[:, :], rstd.t[:, :], ALU.mult, [t1.r, rstd.r],
                       [t1.r])
                    tt("pool", yTv[:, :, c * 64:(c + 1) * 64], v3(t1.t, 8),
                       sgv[:, :, c * 64:(c + 1) * 64], ALU.mult, [t1.r, sg.r], [yT.r])
                return run

            pending = None
            for c in range(nch):
                q, r0 = c // 2, (c % 2) * 64
                pss = [pB.next(), pB.next()]
                for h_ in range(8):
                    mm(pss[h_ // 4][:, (h_ % 4) * 128:(h_ % 4 + 1) * 128],
                       kdtmv[r0:r0 + 64, q, h_ * 128:(h_ + 1) * 128],
                       vtmv[r0:r0 + 64, q, h_ * 128:(h_ + 1) * 128], True, True,
                       [kdtm.r, vtm.r], [pss[h_ // 4].r])
                em = emid.rearrange("p (h c) -> p h c", c=nch)[:, :, c:c + 1]
                tt("dve", Spb[:, :, :], Sst[:, :, :], em.broadcast_to([128, 8, 128]), ALU.mult,
                   [Sst.r, small.r], [Spb.r])
                pso = pO.next()
                for h_ in range(8):
                    mm(pso[:, h_ * 64:(h_ + 1) * 64], vtmv[r0:r0 + 64, q, h_ * 128:(h_ + 1) * 128],
                       scTv[r0:r0 + 64, q, h_ * 64:(h_ + 1) * 64], h_ == 0, False, [vtm.r, scT.r],
                       [pso.r], sgc=True)
                for h_ in range(8):
                    mm(pso[:, h_ * 64:(h_ + 1) * 64], Spb[:, h_, :], qtv[:, h_, c * 64:(c + 1) * 64],
                       False, h_ == 7, [Spb.r, qt.r], [pso.r], sgc=True)
                el = elast.rearrange("p (h c) -> p h c", c=nch)[:, :, c:c + 1]
                tt("dve", Sst[:, :, :], Sst[:, :, :], el.broadcast_to([128, 8, 128]), ALU.mult,
                   [Sst.r, small.r], [Sst.r])
                for k2 in range(2):
                    sv = Sst[:, 4 * k2:4 * k2 + 4, :].rearrange("p h v -> p (h v)")
                    tt("dve", sv, sv, pss[k2][:, :], ALU.add, [Sst.r, pss[k2].r], [Sst.r])
                run_early((16 + nch - 1) // nch)
                if pending is not None:
                    pending()
                pending = gnorm(c, pso)
            run_early(16)
            pending()
            for q in range(nsub):
                for half in range(2):
                    ps = pB.next()
                    for h_ in range(8):
                        mm(ps[:rows, :], yTv[:, h_, q * 128:q * 128 + rows],
                           Wout[h_ // 4][:, h_ % 4, half * 512:(half + 1) * 512],
                           h_ == 0, h_ == 7, [yT.r, U[8 + h_ // 4].r], [ps.r])
                    resid_add(j, h, q, half, ps)
            finish_tile(j, h, dst, dstR)

        pipeline(src, srcR, stageB)
        end_phase()

    def final_phase(src, srcR):
        load_gamma(8)
        for j in range(1, ntiles):
            t0, tw, rows, nsub = tinfo(j)
            h = load_tile(j, src, srcR)
            s = tile_rstd(j, h)
            for q in range(nsub):
                stt(h[:rows, q, :], h[:rows, q, :], s[:rows, 6 + q:7 + q], gbc[:rows, :],
                    ALU.mult, ALU.mult, [h.r, s.r, gbc.r], [h.r])
            o0 = t0 - PRE
            dma("sp", out[o0:o0 + tw, :].rearrange("(s p) d -> p s d", p=rows),
                h[:rows, :nsub, :], [h.r], outR[j])
            out_written.append(outR[j])

    def debug_copy_phase(src, srcR):
        for j in range(ntiles):
            t0, tw, rows, nsub = tinfo(j)
            h = load_tile(j, src, srcR)
            dma("sp", out[t0:t0 + tw, :].rearrange("(s p) d -> p s d", p=rows),
                h[:rows, :nsub, :], [h.r], outR[j])
            out_written.append(outR[j])

    phases = []
    for L in range(4):
        phases.append(("mix", L))
        phases.append(("mlp", L))
    if phase_list is not None:
        phases = list(phase_list)
    src, srcR = hin, None
    if any(k == "mix" and l % 2 == 1 for k, l in phases):
        lb_setup()
    for kind, L in phases:
        if kind == "mlp":
            mlp_phase(L, src, srcR, hbuf, hbufR)
        elif L % 2 == 0:
            even_phase(L, src, srcR, hbuf, hbufR)
        else:
            hgrn_phase(L, src, srcR, hbuf, hbufR)
        src, srcR = hbuf, hbufR
    if debug_out:
        debug_copy_phase(src, srcR)
    else:
        final_phase(src, srcR)
    P.add("sp", None, out_written, ())

    P.finalize()
    with nc.Block() as block:
        @block.tensor
        def _(e):
            P.emit("pe", e)

        @block.scalar
        def _(e):
            P.emit("act", e)

        @block.vector
        def _(e):
            P.emit("dve", e)

        @block.gpsimd
        def _(e):
            P.emit("pool", e)

        @block.sync
        def _(e):
            P.emit("sp", e)
    es.close()
    return nc, P


def _host_consts():
    c = np.zeros((128, CST_W), np.float32)
    c[:, C_ID:C_ID + 128] = np.eye(128, dtype=np.float32)
    p = np.arange(128)[:, None] % 64
    t = np.arange(64)[None, :]
    caus = (p <= t).astype(np.float32)
    c[:, C_CAUS:C_CAUS + 512] = np.tile(caus, (1, 8))
    s = np.arange(128)[:, None]
    tcol = np.arange(128)[None, :]
    for gi, w in enumerate(WINS):
        Dw = ((s <= tcol) & (s >= tcol - w + 1)).astype(np.float32) / w - np.eye(128, dtype=np.float32)
        Ow = ((s - 128) >= (tcol - w + 1)).astype(np.float32) / w
        O1 = np.zeros((128, 128), np.float32)
        O1[:64] = Ow[64:]
        Dpre = np.zeros((128, 128), np.float32)
        for tc in range(48, 64):
            cntv = min(tc - 48 + 1, w)
            for sr in range(max(0, tc - w + 1), tc + 1):
                Dpre[sr, tc] = 1.0 / cntv
            Dpre[tc, tc] -= 1.0
        for k, M in enumerate((Dw, Ow, O1, Dpre)):
            col = C_BAND + (k * 4 + gi) * 128
            c[:, col:col + 128] = M
    return c


def _host_pvec(inp):
    pvv = np.zeros((128, PV_W), np.float32)
    for j in range(2):
        b = PV_EV + j * PV_EV_W
        cw = inp["ev_conv_w"][j]
        pvv[:, b:b + 124] = cw.reshape(31, 4, 128).transpose(2, 1, 0).reshape(128, 124)
        pvv[:, b + 124:b + 128] = inp["ev_conv_b"][j].reshape(4, 128).T
        pvv[:, b + 128:b + 132] = inp["ev_ln_g"][j].reshape(4, 128).T
        pvv[:, b + 132:b + 136] = inp["ev_ln_b"][j].reshape(4, 128).T
        pvv[:, b + 136:b + 140] = inp["ev_pool_b"][j].T
        pvv[:, b + 140:b + 144] = inp["ev_pool_scale"][j].reshape(4, 128).T
        pvv[:, PV_OD + j] = inp["od_gnorm_g"][j]
    pvv[:, PV_LB:PV_LB + 32] = inp["lb_param"].reshape(4, 8, 128).transpose(2, 1, 0).reshape(128, 32)
    return pvv


def _in_maps(inp):
    inp = {k: np.ascontiguousarray(np.asarray(v, dtype=np.float32)) for k, v in inp.items()}
    cstv = _host_consts()
    pvv = _host_pvec(inp)
    gamv = np.concatenate([inp["mix_norm_g"], inp["mlp_norm_g"], inp["final_norm_g"][None]], 0)
    maps = []
    for b in range(BATCH):
        hin = np.zeros((TTOT, D), np.float32)
        hin[PRE - NMETA:PRE] = inp["meta_tokens"]
        hin[PRE:] = inp["x"][b]
        maps.append({
            "hin": hin, "gam": gamv, "cst": cstv, "pvec": pvv,
            "ev_w_in": inp["ev_w_in"], "ev_pool_w": inp["ev_pool_w"], "ev_w_out": inp["ev_w_out"],
            "od_w_in": inp["od_w_in"], "od_w_out": inp["od_w_out"],
            "mlp_w1": inp["mlp_w1"], "mlp_w2": inp["mlp_w2"],
        })
    return maps


def kernel(**inputs):
    nc, _ = build_program()
    maps = _in_maps(inputs)
    res = run_bass_kernel_spmd(nc, maps, core_ids=list(range(NCORES)))
    return np.stack([np.asarray(r["out"], dtype=np.float32) for r in res.results], axis=0)
```
